# Optimizing a Trainium2 kernel written in Bass

```python
import jax
import jax.numpy as jnp
from jax import lax
import numpy as np

D_MODEL = 2048
BATCH = 2
SEQ = 4096
DEPTH = 4

N_MIXERS = 4
N_META = 16
EPS = 1e-6
D_FF = 256 * ((8 * D_MODEL // 3 + 255) // 256)

MLSTM_HEADS = 8
MLSTM_DV = D_MODEL // MLSTM_HEADS
MLSTM_DK = MLSTM_DV // 2
MLSTM_CHUNK = 64
MLSTM_QK_W = MLSTM_HEADS * MLSTM_DK
MLSTM_V_W = MLSTM_HEADS * MLSTM_DV
MLSTM_IN = 2 * MLSTM_QK_W + 2 * MLSTM_V_W + 2 * MLSTM_HEADS

POOL_WINDOWS = (2, 4, 8, 16)
POOL_GROUP = D_MODEL // len(POOL_WINDOWS)

GDN_DK = 128
GDN_DV = 128
GDN_QK_HEADS = D_MODEL // GDN_DK
GDN_V_HEADS = 2 * GDN_QK_HEADS
GDN_CONV = 4
GDN_CHUNK = 64
GDN_QK_W = GDN_QK_HEADS * GDN_DK
GDN_V_W = GDN_V_HEADS * GDN_DV
GDN_CONV_CH = 2 * GDN_QK_W + GDN_V_W
GDN_IN = GDN_CONV_CH + GDN_V_W + 2 * GDN_V_HEADS

SWA_DH = 64
SWA_HQ = D_MODEL // SWA_DH
SWA_GROUP = 8
SWA_HKV = SWA_HQ // SWA_GROUP
SWA_WINDOW = 128
SWA_BLOCK = SWA_WINDOW
SWA_Q_W = SWA_HQ * SWA_DH
SWA_KV_W = SWA_HKV * SWA_DH
SWA_IN = SWA_Q_W + 2 * SWA_KV_W
ROPE_THETA = 10000.0

kernel_name = 'hybrid_interleaved_mlstm_pool_gdn_swa'


def _n_layers_of(mixer):
    return len(range(mixer, DEPTH, N_MIXERS))


def _rms_normalize(x):
    xf = x.astype(jnp.float32)
    return xf * lax.rsqrt(jnp.mean(xf * xf, axis=-1, keepdims=True) + EPS)


def _rmsnorm(x, w):
    return (_rms_normalize(x) * w.astype(jnp.float32)).astype(x.dtype)


def _swiglu(x, w_gate, w_up, w_down):
    return (jax.nn.silu(x @ w_gate) * (x @ w_up)) @ w_down


def _to_chunks(t, c):
    t = t.reshape(t.shape[0], t.shape[1], t.shape[2] // c, c, *t.shape[3:])
    return jnp.moveaxis(t, 2, 0)


def _from_chunks(t):
    t = jnp.moveaxis(t, 0, 2)
    return t.reshape(t.shape[0], t.shape[1], -1, *t.shape[4:])


def _meta_then_chunks(step, state, seqs, chunk):
    meta = tuple(s[:, :, :N_META] for s in seqs)
    real = tuple(_to_chunks(s[:, :, N_META:], chunk) for s in seqs)
    state, out_meta = step(state, meta)
    _, out_real = lax.scan(step, state, real)
    return jnp.concatenate([out_meta, _from_chunks(out_real)], axis=2)


def _mlstm_chunk(state, xs):
    c_st, n_st, m_st = state
    q, k, v, li, lf = xs
    c = q.shape[2]
    causal = jnp.tril(jnp.ones((c, c), bool))
    b = jnp.cumsum(lf, axis=-1)
    log_w = jnp.where(causal, b[..., :, None] - b[..., None, :] + li[..., None, :], -jnp.inf)
    log_init = b + m_st[..., None]
    m_t = jnp.maximum(log_init, jnp.max(log_w, axis=-1))
    w = jnp.exp(log_w - m_t[..., None])
    w_init = jnp.exp(log_init - m_t)
    qk = jnp.einsum('bhtd,bhsd->bhts', q, k) * w
    num = w_init[..., None] * jnp.einsum('bhtd,bhde->bhte', q, c_st) + jnp.einsum('bhts,bhse->bhte', qk, v)
    den = w_init * jnp.einsum('bhtd,bhd->bht', q, n_st) + jnp.sum(qk, axis=-1)
    h = num / jnp.maximum(jnp.abs(den), jnp.exp(-m_t))[..., None]
    log_end_init = b[..., -1] + m_st
    log_end = b[..., -1:] - b + li
    m_new = jnp.maximum(log_end_init, jnp.max(log_end, axis=-1))
    a_init = jnp.exp(log_end_init - m_new)
    a = jnp.exp(log_end - m_new[..., None])
    c_new = a_init[..., None, None] * c_st + jnp.einsum('bhs,bhsd,bhse->bhde', a, k, v)
    n_new = a_init[..., None] * n_st + jnp.einsum('bhs,bhsd->bhd', a, k)
    return (c_new, n_new, m_new), h


def _mlstm(u, w_in, b_if, norm_w, w_out):
    bsz, L, _ = u.shape
    p = u @ w_in
    q, k, v, og, gates = jnp.split(p, [MLSTM_QK_W, 2 * MLSTM_QK_W, 2 * MLSTM_QK_W + MLSTM_V_W,
                                       2 * MLSTM_QK_W + 2 * MLSTM_V_W], axis=-1)
    heads = lambda t, d: t.reshape(bsz, L, MLSTM_HEADS, d).transpose(0, 2, 1, 3).astype(jnp.float32)
    q = heads(q, MLSTM_DK) * (MLSTM_DK ** -0.5)
    k = heads(k, MLSTM_DK)
    v = heads(v, MLSTM_DV)
    gates = (gates.astype(jnp.float32) + b_if.astype(jnp.float32)).transpose(0, 2, 1)
    li = gates[:, :MLSTM_HEADS]
    lf = jax.nn.log_sigmoid(gates[:, MLSTM_HEADS:])
    state0 = (jnp.zeros((bsz, MLSTM_HEADS, MLSTM_DK, MLSTM_DV), jnp.float32),
              jnp.zeros((bsz, MLSTM_HEADS, MLSTM_DK), jnp.float32),
              jnp.zeros((bsz, MLSTM_HEADS), jnp.float32))
    hh = _meta_then_chunks(_mlstm_chunk, state0, (q, k, v, li, lf), MLSTM_CHUNK)
    hh = _rms_normalize(hh.transpose(0, 2, 1, 3)).reshape(bsz, L, MLSTM_V_W)
    hh = hh * norm_w.astype(jnp.float32) * jax.nn.sigmoid(og.astype(jnp.float32))
    return hh.astype(u.dtype) @ w_out


def _pool_mixer(u, w_group, scale):
    bsz, L, _ = u.shape
    uf = u.astype(jnp.float32)
    cs = jnp.cumsum(uf, axis=1)
    count = jnp.arange(L) + 1
    outs = []
    for gi, win in enumerate(POOL_WINDOWS):
        lo, hi = gi * POOL_GROUP, (gi + 1) * POOL_GROUP
        c = cs[..., lo:hi]
        lag = jnp.pad(c[:, :L - win], ((0, 0), (win, 0), (0, 0)))
        mean = (c - lag) / jnp.minimum(count, win).astype(jnp.float32)[None, :, None]
        outs.append(mean - uf[..., lo:hi])
    pooled = jnp.stack(outs, axis=2).astype(u.dtype)
    y = jnp.einsum('blgc,gcd->blgd', pooled, w_group).reshape(bsz, L, D_MODEL)
    return y * scale


def _causal_depthwise_conv(x, w):
    return lax.conv_general_dilated(x, w[:, None, :], window_strides=(1,), padding=[(w.shape[0] - 1, 0)],
                                    dimension_numbers=('NWC', 'WIO', 'NWC'), feature_group_count=x.shape[-1])


def _l2norm(x):
    return x * lax.rsqrt(jnp.sum(x * x, axis=-1, keepdims=True) + EPS)


def _gdn_chunk(s_st, xs):
    q, k, v, g, beta = xs
    c = q.shape[2]
    causal = jnp.tril(jnp.ones((c, c), bool))
    strict = jnp.tril(jnp.ones((c, c), bool), -1)
    gc = jnp.cumsum(g, axis=-1)
    decay = jnp.exp(jnp.where(causal, gc[..., :, None] - gc[..., None, :], -jnp.inf))
    kb = k * beta[..., None]
    lower = jnp.where(strict, jnp.einsum('bhtd,bhsd->bhts', kb, k) * decay, 0.0)
    a_mat = lower + jnp.eye(c, dtype=lower.dtype)
    rhs = jnp.concatenate([v * beta[..., None], kb * jnp.exp(gc)[..., None]], axis=-1)
    sol = lax.linalg.triangular_solve(a_mat, rhs, left_side=True, lower=True, unit_diagonal=True)
    u_vec, w_vec = sol[..., :GDN_DV], sol[..., GDN_DV:]
    v_new = u_vec - jnp.einsum('bhtd,bhde->bhte', w_vec, s_st)
    attn = jnp.einsum('bhtd,bhsd->bhts', q, k) * decay
    o = jnp.einsum('bhtd,bhde->bhte', q * jnp.exp(gc)[..., None], s_st) + jnp.einsum('bhts,bhse->bhte', attn, v_new)
    g_last = gc[..., -1]
    s_new = jnp.exp(g_last)[..., None, None] * s_st + jnp.einsum(
        'bhsd,bhse->bhde', k * jnp.exp(g_last[..., None] - gc)[..., None], v_new)
    return s_new, o


def _gated_deltanet(u, w_in, conv_w, a_log, dt_bias, norm_w, w_out):
    bsz, L, _ = u.shape
    p = u @ w_in
    qkv, z, b_pre, a_pre = jnp.split(p, [GDN_CONV_CH, GDN_CONV_CH + GDN_V_W, GDN_CONV_CH + GDN_V_W + GDN_V_HEADS], axis=-1)
    qkv = jax.nn.silu(_causal_depthwise_conv(qkv, conv_w))
    q, k, v = jnp.split(qkv, [GDN_QK_W, 2 * GDN_QK_W], axis=-1)
    heads = lambda t, n, d: t.reshape(bsz, L, n, d).transpose(0, 2, 1, 3).astype(jnp.float32)
    rep = GDN_V_HEADS // GDN_QK_HEADS
    q = jnp.repeat(_l2norm(heads(q, GDN_QK_HEADS, GDN_DK)) * (GDN_DK ** -0.5), rep, axis=1)
    k = jnp.repeat(_l2norm(heads(k, GDN_QK_HEADS, GDN_DK)), rep, axis=1)
    v = heads(v, GDN_V_HEADS, GDN_DV)
    beta = jax.nn.sigmoid(b_pre.astype(jnp.float32)).transpose(0, 2, 1)
    g = (-jnp.exp(a_log.astype(jnp.float32))
         * jax.nn.softplus(a_pre.astype(jnp.float32) + dt_bias.astype(jnp.float32))).transpose(0, 2, 1)
    s0 = jnp.zeros((bsz, GDN_V_HEADS, GDN_DK, GDN_DV), jnp.float32)
    o = _meta_then_chunks(_gdn_chunk, s0, (q, k, v, g, beta), GDN_CHUNK)
    o = _rms_normalize(o.transpose(0, 2, 1, 3)) * norm_w.astype(jnp.float32)
    o = o * jax.nn.silu(z.astype(jnp.float32).reshape(bsz, L, GDN_V_HEADS, GDN_DV))
    return o.reshape(bsz, L, GDN_V_W).astype(u.dtype) @ w_out


def _rope_tables(L, d):
    inv = ROPE_THETA ** (-jnp.arange(0, d, 2, dtype=jnp.float32) / d)
    ang = jnp.arange(L, dtype=jnp.float32)[:, None] * inv[None, :]
    ang = jnp.concatenate([ang, ang], axis=-1)
    return jnp.cos(ang), jnp.sin(ang)


def _rope(x, cos, sin):
    shape = (1, x.shape[1]) + (1,) * (x.ndim - 3) + (x.shape[-1],)
    cos, sin = cos.reshape(shape), sin.reshape(shape)
    x1, x2 = jnp.split(x, 2, axis=-1)
    return x * cos + jnp.concatenate([-x2, x1], axis=-1) * sin


def _swa_sinks(u, w_qkv, b_qkv, sinks, w_out, b_out):
    bsz, L, _ = u.shape
    p = (u @ w_qkv + b_qkv).astype(jnp.float32)
    q, k, v = jnp.split(p, [SWA_Q_W, SWA_Q_W + SWA_KV_W], axis=-1)
    q = q.reshape(bsz, L, SWA_HKV, SWA_GROUP, SWA_DH)
    k = k.reshape(bsz, L, SWA_HKV, SWA_DH)
    v = v.reshape(bsz, L, SWA_HKV, SWA_DH)
    cos, sin = _rope_tables(L, SWA_DH)
    q, k = _rope(q, cos, sin), _rope(k, cos, sin)
    nb = -(-L // SWA_BLOCK)
    lp = nb * SWA_BLOCK
    pad_end = lambda t: jnp.pad(t, ((0, 0), (0, lp - L)) + ((0, 0),) * (t.ndim - 2))
    q, k, v = pad_end(q), pad_end(k), pad_end(v)
    qb = q.reshape(bsz, nb, SWA_BLOCK, SWA_HKV, SWA_GROUP, SWA_DH)

    def band(t):
        tp = jnp.pad(t, ((0, 0), (SWA_BLOCK, 0), (0, 0), (0, 0)))
        prev = tp[:, :lp].reshape(bsz, nb, SWA_BLOCK, SWA_HKV, SWA_DH)
        cur = tp[:, SWA_BLOCK:].reshape(bsz, nb, SWA_BLOCK, SWA_HKV, SWA_DH)
        return jnp.concatenate([prev, cur], axis=2)

    kb, vb = band(k), band(v)
    s = jnp.einsum('bnqhgd,bnkhd->bnhgqk', qb, kb) * (SWA_DH ** -0.5)
    blk = jnp.arange(nb)[:, None, None] * SWA_BLOCK
    qpos = blk + jnp.arange(SWA_BLOCK)[None, :, None]
    kpos = blk - SWA_BLOCK + jnp.arange(2 * SWA_BLOCK)[None, None, :]
    mask = (kpos <= qpos) & (qpos - kpos < SWA_WINDOW) & (kpos >= 0)
    s = jnp.where(mask[None, :, None, None], s, -jnp.inf)
    sink = jnp.broadcast_to(sinks.astype(jnp.float32).reshape(1, 1, SWA_HKV, SWA_GROUP, 1, 1), s.shape[:-1] + (1,))
    prob = jax.nn.softmax(jnp.concatenate([s, sink], axis=-1), axis=-1)[..., :-1]
    o = jnp.einsum('bnhgqk,bnkhd->bnqhgd', prob, vb).reshape(bsz, lp, SWA_Q_W)[:, :L]
    return o.astype(u.dtype) @ w_out + b_out


def setup_inputs(seed: int = 0) -> dict:
    key = jax.random.key(seed)
    ks = iter(jax.random.split(key, 40))
    nrm = lambda shape, s=1.0: s * jax.random.normal(next(ks), shape, jnp.float32)
    dense = lambda shape: nrm(shape, shape[-2] ** -0.5)
    gain = lambda shape: 1.0 + nrm(shape, 0.02)
    na, nb, nc, nd = (_n_layers_of(m) for m in range(N_MIXERS))
    x = nrm((BATCH, SEQ, D_MODEL))
    meta_tokens = nrm((N_META, D_MODEL))
    norm_w = gain((DEPTH, 3, D_MODEL))
    ffn_w_gate = dense((DEPTH, 2, D_MODEL, D_FF))
    ffn_w_up = dense((DEPTH, 2, D_MODEL, D_FF))
    ffn_w_down = dense((DEPTH, 2, D_FF, D_MODEL))
    mlstm_w_in = dense((na, D_MODEL, MLSTM_IN))
    b_i = nrm((na, MLSTM_HEADS), 0.1)
    b_f = jnp.linspace(3.0, 6.0, MLSTM_HEADS, dtype=jnp.float32)[None, :] + nrm((na, MLSTM_HEADS), 0.1)
    mlstm_b_if = jnp.concatenate([b_i, b_f], axis=-1)
    mlstm_norm_w = gain((na, MLSTM_V_W))
    mlstm_w_out = dense((na, MLSTM_V_W, D_MODEL))
    pool_w = dense((nb, len(POOL_WINDOWS), POOL_GROUP, POOL_GROUP))
    pool_scale = gain((nb, D_MODEL))
    gdn_w_in = dense((nc, D_MODEL, GDN_IN))
    gdn_conv_w = nrm((nc, GDN_CONV, GDN_CONV_CH), GDN_CONV ** -0.5)
    gdn_a_log = jnp.log(jax.random.uniform(next(ks), (nc, GDN_V_HEADS), jnp.float32, 1.0, 16.0))
    dt = jnp.exp(jax.random.uniform(next(ks), (nc, GDN_V_HEADS), jnp.float32,
                                    float(np.log(1e-3)), float(np.log(1e-1))))
    gdn_dt_bias = dt + jnp.log(-jnp.expm1(-dt))
    gdn_norm_w = gain((nc, GDN_DV))
    gdn_w_out = dense((nc, GDN_V_W, D_MODEL))
    swa_w_qkv = dense((nd, D_MODEL, SWA_IN))
    swa_b_qkv = nrm((nd, SWA_IN), 0.02)
    swa_sinks = nrm((nd, SWA_HQ), 0.5)
    swa_w_out = dense((nd, SWA_Q_W, D_MODEL))
    swa_b_out = nrm((nd, D_MODEL), 0.02)
    final_norm_w = gain((D_MODEL,))
    return {'x': x, 'meta_tokens': meta_tokens, 'norm_w': norm_w,
            'ffn_w_gate': ffn_w_gate, 'ffn_w_up': ffn_w_up, 'ffn_w_down': ffn_w_down,
            'mlstm_w_in': mlstm_w_in, 'mlstm_b_if': mlstm_b_if, 'mlstm_norm_w': mlstm_norm_w, 'mlstm_w_out': mlstm_w_out,
            'pool_w': pool_w, 'pool_scale': pool_scale,
            'gdn_w_in': gdn_w_in, 'gdn_conv_w': gdn_conv_w, 'gdn_a_log': gdn_a_log, 'gdn_dt_bias': gdn_dt_bias,
            'gdn_norm_w': gdn_norm_w, 'gdn_w_out': gdn_w_out,
            'swa_w_qkv': swa_w_qkv, 'swa_b_qkv': swa_b_qkv, 'swa_sinks': swa_sinks, 'swa_w_out': swa_w_out,
            'swa_b_out': swa_b_out, 'final_norm_w': final_norm_w}


def reference(x, meta_tokens, norm_w, ffn_w_gate, ffn_w_up, ffn_w_down,
              mlstm_w_in, mlstm_b_if, mlstm_norm_w, mlstm_w_out,
              pool_w, pool_scale,
              gdn_w_in, gdn_conv_w, gdn_a_log, gdn_dt_bias, gdn_norm_w, gdn_w_out,
              swa_w_qkv, swa_b_qkv, swa_sinks, swa_w_out, swa_b_out, final_norm_w):
    bsz = x.shape[0]
    meta = jnp.broadcast_to(meta_tokens.astype(x.dtype)[None], (bsz, N_META, D_MODEL))
    h = jnp.concatenate([meta, x], axis=1)
    for i in range(DEPTH):
        m, j = i % N_MIXERS, i // N_MIXERS
        h = h + 0.5 * _swiglu(_rmsnorm(h, norm_w[i, 0]), ffn_w_gate[i, 0], ffn_w_up[i, 0], ffn_w_down[i, 0])
        u = _rmsnorm(h, norm_w[i, 1])
        if m == 0:
            y = _mlstm(u, mlstm_w_in[j], mlstm_b_if[j], mlstm_norm_w[j], mlstm_w_out[j])
        elif m == 1:
            y = _pool_mixer(u, pool_w[j], pool_scale[j])
        elif m == 2:
            y = _gated_deltanet(u, gdn_w_in[j], gdn_conv_w[j], gdn_a_log[j], gdn_dt_bias[j], gdn_norm_w[j], gdn_w_out[j])
        else:
            y = _swa_sinks(u, swa_w_qkv[j], swa_b_qkv[j], swa_sinks[j], swa_w_out[j], swa_b_out[j])
        h = h + y
        h = h + 0.5 * _swiglu(_rmsnorm(h, norm_w[i, 2]), ffn_w_gate[i, 1], ffn_w_up[i, 1], ffn_w_down[i, 1])
    return _rmsnorm(h, final_norm_w)[:, N_META:]
```

```python
from contextlib import ExitStack
import numpy as np
import concourse.bass as bass
import concourse.mybir as mybir

F32 = mybir.dt.float32
BF16 = mybir.dt.bfloat16
AF = mybir.ActivationFunctionType
ALU = mybir.AluOpType
AX = mybir.AxisListType

ENGS = ('pe', 'act', 'dve', 'pool', 'sp')


import types


def freeze(fn):
    if fn is None or fn.__closure__ is None:
        return fn
    cells = []
    for c in fn.__closure__:
        try:
            cells.append(types.CellType(c.cell_contents))
        except ValueError:
            cells.append(c)
    return types.FunctionType(fn.__code__, fn.__globals__, fn.__name__, fn.__defaults__, tuple(cells))


class T:
    __slots__ = ('ap', 'w', 'r', 'name')

    def __init__(self, ap, name=''):
        self.ap = ap
        self.w = None
        self.r = []
        self.name = name

    def __getitem__(self, idx):
        return self.ap[idx]


class Prog:
    def __init__(self, nc, same_eng_sync=True):
        self.nc = nc
        self.es = ExitStack()
        self.q = {e: [] for e in ENGS}
        self.cnt = {}
        self.sem = {}
        self.seen = {e: {} for e in ENGS}
        self.same_eng_sync = same_eng_sync
        self.n_ops = 0
        self.dcount = {}
        self.KD = 8
        for e in ENGS:
            self._mksem('e_' + e)

    def _mksem(self, name):
        if name not in self.sem:
            self.sem[name] = self.es.enter_context(self.nc.semaphore(name))
            self.cnt[name] = 0

    def sb(self, name, shape, dtype=F32):
        return self.es.enter_context(self.nc.sbuf_tensor(name, list(shape), dtype))

    def ps(self, name, shape, dtype=F32):
        return self.es.enter_context(self.nc.psum_tensor(name, list(shape), dtype))

    def tile(self, name, shape, dtype=F32):
        t = self.sb(name, shape, dtype)
        return T(t, name)

    def ptile(self, name, shape, dtype=F32):
        t = self.ps(name, shape, dtype)
        return T(t, name)

    def op(self, eng, fn, reads=(), writes=(), stream=None):
        waits = {}

        def need(ev, is_raw):
            if ev is None:
                return
            s, v = ev
            if s == 'e_' + eng and stream is None:
                if eng == 'pe' or not (self.same_eng_sync and is_raw):
                    return
            if self.seen[eng].get(s, 0) >= v:
                return
            if waits.get(s, 0) < v:
                waits[s] = v

        for t in reads:
            need(t.w, True)
        for t in writes:
            need(t.w, False)
            for ev in t.r:
                need(ev, False)
        for s, v in waits.items():
            self.seen[eng][s] = v
        if stream is None:
            s = 'e_' + eng
            inc = 1
        else:
            k = self.dcount.get(stream, 0)
            self.dcount[stream] = k + 1
            s = 'd_%s_%d' % (stream, k % self.KD)
            self._mksem(s)
            inc = 16
            if self.cnt[s] > 0 and self.seen[eng].get(s, 0) < self.cnt[s]:
                waits[s] = self.cnt[s]
                self.seen[eng][s] = self.cnt[s]
        self.cnt[s] += inc
        ev = (s, self.cnt[s])
        for t in reads:
            t.r.append(ev)
        for t in writes:
            t.w = ev
            t.r = []
        self.q[eng].append((list(waits.items()), freeze(fn), s, inc))
        self.n_ops += 1
        return ev

    def finish(self):
        fw = []
        for s, c in self.cnt.items():
            if s.startswith('d_') and c > 0:
                fw.append((s, c))
        for e in ENGS:
            if e != 'sp' and self.cnt['e_' + e] > 0:
                fw.append(('e_' + e, self.cnt['e_' + e]))
        self.q['sp'].append((fw, None, None, 0))
        nc = self.nc
        sem = self.sem
        q = self.q

        def replay(name, e):
            for waits, fn, s, inc in q[name]:
                for ws, wv in waits:
                    e.wait_ge(sem[ws], wv)
                if fn is not None:
                    ins = fn(e)
                    ins.then_inc(sem[s], inc)

        with nc.Block() as block:
            @block.tensor
            def _(e):
                replay('pe', e)

            @block.scalar
            def _(e):
                replay('act', e)

            @block.vector
            def _(e):
                replay('dve', e)

            @block.gpsimd
            def _(e):
                replay('pool', e)

            @block.sync
            def _(e):
                replay('sp', e)
        self.es.close()


D = 2048
DFF = 5632
NT = 1040
NTT = 9
FB = 2
EPS = 1e-6


def tok_tiles():
    return [(tt * 128, 128 if tt < 8 else 16) for tt in range(NTT)]


def tok_halves():
    return [(0, 512), (512, 512), (1024, 16)]


class TCtx:
    def __init__(self, P, nc):
        self.P = P
        self.nc = nc
        self.h_sb = P.sb("h_sb", [128, NTT, D], F32)
        self.h = [[T(self.h_sb[0:np_, tt, ds * 512:(ds + 1) * 512], f"h{tt}_{ds}") for ds in range(4)]
                  for tt, (t0, np_) in enumerate(tok_tiles())]
        self.xnT_sb = P.sb("xnT", [128, 16, NT], BF16)
        self.xnT = [T(self.xnT_sb[:, :, t0:t0 + np_], f"xnT{tt}") for tt, (t0, np_) in enumerate(tok_tiles())]
        self.NW = 3
        self.wg = [P.tile(f"wg{i}", [128, 16, FB * 128], BF16) for i in range(2)]
        self.wu = [P.tile(f"wu{i}", [128, 16, FB * 128], BF16) for i in range(2)]
        self.gslot = 0
        self.wd = [P.tile(f"wd{i}", [128, FB, D], BF16) for i in range(self.NW)]
        self.wslot = 0
        self.actT = [P.tile(f"actT{i}", [128, FB, NT], BF16) for i in range(2)]
        self.aslot = 0
        self.sg = [P.tile(f"sg{i}", [128, 512], F32) for i in range(2)]
        self.sgi = 0
        self.xs = [P.tile(f"xs{i}", [128, D], BF16) for i in range(2)]
        self.xsi = 0
        self.wbc = [P.tile(f"wbc{i}", [128, D], F32) for i in range(2)]
        self.wbci = 0
        self.ss = [P.tile(f"ss{i}", [128, 1], F32) for i in range(4)]
        self.ssi = 0
        self.tmp = [P.tile(f"tmp{i}", [128, 512], F32) for i in range(2)]
        self.tmpi = 0
        self.ident_f = P.tile("ident_f", [128, 128], F32)
        self.ident = P.tile("ident_b", [128, 128], BF16)
        self.pg = [P.ptile(f"pg{i}", [128, 512], F32) for i in range(2)]
        self.pu = [P.ptile(f"pu{i}", [128, 512], F32) for i in range(2)]
        self.pgi = 0
        self.py = [P.ptile(f"py{i}", [128, 512], F32) for i in range(3)]
        self.pyi = 0
        self.ptr = P.ptile("ptr", [128, 8 * 128], BF16)

    def load_ident(self, ident_dram):
        P = self.P
        P.op('sp', lambda e: e.dma_start(out=self.ident_f[:], in_=ident_dram), writes=[self.ident_f], stream='c')
        P.op('dve', lambda e: e.tensor_copy(self.ident[:], self.ident_f[:]), reads=[self.ident_f], writes=[self.ident])

    def load_h(self, h_dram):
        P = self.P
        for tt, (t0, np_) in enumerate(tok_tiles()):
            P.op('sp', lambda e, tt=tt, t0=t0, np_=np_: e.dma_start(out=self.h_sb[0:np_, tt, :], in_=h_dram[t0:t0 + np_, :]),
                 writes=self.h[tt], stream='a')

    def store_h(self, h_dram):
        P = self.P
        for tt, (t0, np_) in enumerate(tok_tiles()):
            P.op('sp', lambda e, tt=tt, t0=t0, np_=np_: e.dma_start(out=h_dram[t0:t0 + np_, :], in_=self.h_sb[0:np_, tt, :]),
                 reads=self.h[tt], stream='o')

    def norm(self, normw_dram, mode, out_dram=None):
        P = self.P
        wbc = self.wbc[self.wbci]
        self.wbci ^= 1
        P.op('sp', lambda e: e.dma_start(out=wbc[:], in_=normw_dram), writes=[wbc], stream='c')
        for tt, (t0, np_) in enumerate(tok_tiles()):
            if mode == 'out' and tt == 8:
                continue
            xs = self.xs[self.xsi]
            self.xsi ^= 1
            ss = self.ss[self.ssi]
            self.ssi = (self.ssi + 1) % 4
            hfull = self.h_sb[0:np_, tt, :]
            P.op('act', lambda e, xs=xs, ss=ss, hfull=hfull, np_=np_: e.activation(xs[0:np_, :], hfull, AF.Square, accum_out=ss[0:np_, :]),
                 reads=self.h[tt], writes=[xs, ss])
            P.op('dve', lambda e, ss=ss, np_=np_: e.tensor_scalar(ss[0:np_, :], ss[0:np_, :], 1.0 / D, EPS, ALU.mult, ALU.add),
                 reads=[ss], writes=[ss])
            P.op('act', lambda e, ss=ss, np_=np_: e.activation(ss[0:np_, :], ss[0:np_, :], AF.Sqrt),
                 reads=[ss], writes=[ss])
            P.op('dve', lambda e, ss=ss, np_=np_: e.reciprocal(ss[0:np_, :], ss[0:np_, :]),
                 reads=[ss], writes=[ss])
            if mode == 'T':
                P.op('dve', lambda e, xs=xs, ss=ss, hfull=hfull, np_=np_: e.scalar_tensor_tensor(
                    xs[0:np_, :], hfull, ss[0:np_, 0:1], wbc[0:np_, :], ALU.mult, ALU.mult),
                    reads=self.h[tt] + [ss, wbc], writes=[xs])
                for half in range(2):
                    for j in range(8):
                        kc = half * 8 + j
                        P.op('pe', lambda e, xs=xs, j=j, kc=kc, np_=np_: e.transpose(
                            self.ptr[:, j * 128:j * 128 + np_], xs[0:np_, kc * 128:(kc + 1) * 128], self.ident[0:np_, 0:np_]),
                            reads=[xs, self.ident], writes=[self.ptr])
                    eng = 'act' if half == 0 else 'dve'
                    src = self.ptr[:, :].rearrange("p (j t) -> p j t", t=128)[:, :, 0:np_]
                    dst = self.xnT_sb[:, half * 8:(half + 1) * 8, t0:t0 + np_]
                    if eng == 'act':
                        P.op('act', lambda e, src=src, dst=dst: e.copy(dst, src), reads=[self.ptr], writes=[self.xnT[tt]])
                    else:
                        P.op('dve', lambda e, src=src, dst=dst: e.tensor_copy(dst, src), reads=[self.ptr], writes=[self.xnT[tt]])
            else:
                xf = self.wbc[self.wbci]
                P.op('dve', lambda e, xf=xf, ss=ss, hfull=hfull, np_=np_: e.scalar_tensor_tensor(
                    xf[0:np_, :], hfull, ss[0:np_, 0:1], wbc[0:np_, :], ALU.mult, ALU.mult),
                    reads=self.h[tt] + [ss, wbc], writes=[xf])
                P.op('sp', lambda e, xf=xf, t0=t0, np_=np_: e.dma_start(out=out_dram[t0:t0 + np_, :], in_=xf[0:np_, :]),
                     reads=[xf], stream='o')

    def store_uT(self, uT_dram):
        P = self.P
        dst = uT_dram.rearrange("(kc p) t -> p kc t", p=128)
        for tt, (t0, np_) in enumerate(tok_tiles()):
            P.op('sp', lambda e, t0=t0, np_=np_: e.dma_start(out=dst[:, :, t0:t0 + np_], in_=self.xnT_sb[:, :, t0:t0 + np_]),
                 reads=[self.xnT[tt]], stream='o')

    def stage2(self, actT, wd, scale, nfc=FB, colscale=None):
        P = self.P
        for tt, (t0, np_) in enumerate(tok_tiles()):
            for ds in range(4):
                py = self.py[self.pyi]
                self.pyi = (self.pyi + 1) % 3
                for fc in range(nfc):
                    P.op('pe', lambda e, py=py, fc=fc, t0=t0, np_=np_, ds=ds: e.matmul(
                        py[0:np_, :], actT[:, fc, t0:t0 + np_], wd[:, fc, ds * 512:(ds + 1) * 512],
                        start=(fc == 0), stop=(fc == nfc - 1)),
                        reads=[actT, wd], writes=[py])
                hT = self.h[tt][ds]
                if colscale is None:
                    P.op('dve', lambda e, py=py, hT=hT, np_=np_: e.scalar_tensor_tensor(
                        hT.ap, py[0:np_, :], float(scale), hT.ap, ALU.mult, ALU.add),
                        reads=[py, hT], writes=[hT])
                else:
                    tmp = self.tmp[self.tmpi]
                    self.tmpi ^= 1
                    P.op('dve', lambda e, py=py, tmp=tmp, np_=np_, ds=ds: e.tensor_tensor(
                        tmp[0:np_, :], py[0:np_, :], colscale[0:np_, ds * 512:(ds + 1) * 512], ALU.mult),
                        reads=[py, colscale], writes=[tmp])
                    P.op('dve', lambda e, tmp=tmp, hT=hT, np_=np_: e.tensor_tensor(
                        hT.ap, tmp[0:np_, :], hT.ap, ALU.add),
                        reads=[tmp, hT], writes=[hT])

    def stage1(self, wg, wu, actT):
        P = self.P
        for fc in range(FB):
            for (c0, n) in tok_halves():
                pg = self.pg[self.pgi]
                pu = self.pu[self.pgi]
                self.pgi ^= 1
                tts = sorted(set(range(c0 // 128, (c0 + n + 127) // 128)))
                xr = [self.xnT[tt] for tt in tts]
                for kc in range(16):
                    P.op('pe', lambda e, pg=pg, kc=kc, fc=fc, c0=c0, n=n: e.matmul(
                        pg[:, 0:n], wg[:, kc, fc * 128:(fc + 1) * 128], self.xnT_sb[:, kc, c0:c0 + n],
                        start=(kc == 0), stop=(kc == 15)), reads=[wg] + xr, writes=[pg])
                for kc in range(16):
                    P.op('pe', lambda e, pu=pu, kc=kc, fc=fc, c0=c0, n=n: e.matmul(
                        pu[:, 0:n], wu[:, kc, fc * 128:(fc + 1) * 128], self.xnT_sb[:, kc, c0:c0 + n],
                        start=(kc == 0), stop=(kc == 15)), reads=[wu] + xr, writes=[pu])
                sg = self.sg[self.sgi]
                self.sgi ^= 1
                P.op('act', lambda e, sg=sg, pg=pg, n=n: e.activation(sg[:, 0:n], pg[:, 0:n], AF.Silu),
                     reads=[pg], writes=[sg])
                P.op('dve', lambda e, sg=sg, pu=pu, fc=fc, c0=c0, n=n: e.tensor_tensor(
                    actT[:, fc, c0:c0 + n], sg[:, 0:n], pu[:, 0:n], ALU.mult),
                    reads=[sg, pu], writes=[actT])

    def ffn(self, normw_dram, wg_dram, wu_dram, wd_dram):
        P = self.P
        self.norm(normw_dram, 'T')
        nblk = DFF // (FB * 128)
        wgv = wg_dram.rearrange("(kc p) f -> p kc f", p=128)
        wuv = wu_dram.rearrange("(kc p) f -> p kc f", p=128)
        wdv = wd_dram.rearrange("(fc p) d -> p fc d", p=128)
        slots = []

        def load(b):
            s = self.wslot
            self.wslot = (self.wslot + 1) % self.NW
            wg, wu, wd = self.wg[self.gslot], self.wu[self.gslot], self.wd[s]
            self.gslot ^= 1
            f0 = b * FB * 128
            P.op('pool', lambda e: e.dma_start(out=wg[:], in_=wgv[:, :, f0:f0 + FB * 128]), writes=[wg], stream='w')
            P.op('pool', lambda e: e.dma_start(out=wu[:], in_=wuv[:, :, f0:f0 + FB * 128]), writes=[wu], stream='w')
            P.op('pool', lambda e: e.dma_start(out=wd[:], in_=wdv[:, b * FB:(b + 1) * FB, :]), writes=[wd], stream='w')
            slots.append((wg, wu, wd))

        acts = []
        for b in range(nblk + 1):
            if b < nblk:
                load(b)
                a = self.actT[self.aslot]
                self.aslot ^= 1
                acts.append(a)
                self.stage1(slots[b][0], slots[b][1], a)
            if b >= 1:
                self.stage2(acts[b - 1], slots[b - 1][2], 0.5)

    def outproj(self, oT_dram, w_dram, nF, colscale=None, bias_dram=None):
        P = self.P
        nblk = nF // (FB * 128)
        ov = oT_dram.rearrange("(fc p) t -> p fc t", p=128)
        wv = w_dram.rearrange("(fc p) d -> p fc d", p=128)
        cs = None
        if colscale is not None:
            cs = self.wbc[self.wbci]
            self.wbci ^= 1
            P.op('sp', lambda e: e.dma_start(out=cs[:], in_=colscale), writes=[cs], stream='c')
        for b in range(nblk):
            s = self.wslot
            self.wslot = (self.wslot + 1) % self.NW
            wd = self.wd[s]
            a = self.actT[self.aslot]
            self.aslot ^= 1
            P.op('pool', lambda e, wd=wd, b=b: e.dma_start(out=wd[:], in_=wv[:, b * FB:(b + 1) * FB, :]), writes=[wd], stream='w')
            P.op('sp', lambda e, a=a, b=b: e.dma_start(out=a[:], in_=ov[:, b * FB:(b + 1) * FB, :]), writes=[a], stream='a')
            self.stage2(a, wd, 1.0, colscale=cs)
        if bias_dram is not None:
            bb = self.wbc[self.wbci]
            self.wbci ^= 1
            P.op('sp', lambda e: e.dma_start(out=bb[:], in_=bias_dram), writes=[bb], stream='c')
            for tt, (t0, np_) in enumerate(tok_tiles()):
                for ds in range(4):
                    hT = self.h[tt][ds]
                    P.op('dve', lambda e, hT=hT, np_=np_, ds=ds: e.tensor_tensor(
                        hT.ap, hT.ap, bb[0:np_, ds * 512:(ds + 1) * 512], ALU.add),
                        reads=[hT, bb], writes=[hT])


def build_T(kind, F_o=2048, proj_cols=0):
    nc = bass.Bass("TRN2", target_bir_lowering=False)
    dt = lambda name, shape, dtype=F32: nc.dram_tensor(name, list(shape), dtype, kind="ExternalInput").ap()
    do = lambda name, shape, dtype=F32: nc.dram_tensor(name, list(shape), dtype, kind="ExternalOutput").ap()
    ident = dt("ident", [128, 128])
    h_in = dt("h_in", [NT, D])
    if kind > 0:
        oT = dt("oT", [F_o, NT], BF16)
        w_out = dt("w_out", [F_o, D])
        vec_out = dt("vec_out", [128, D])
        nA = dt("nA", [128, D]); wgA = dt("wgA", [D, DFF]); wuA = dt("wuA", [D, DFF]); wdA = dt("wdA", [DFF, D])
    if kind < 4:
        nB = dt("nB", [128, D]); wgB = dt("wgB", [D, DFF]); wuB = dt("wuB", [D, DFF]); wdB = dt("wdB", [DFF, D])
        nU = dt("nU", [128, D])
        h_out = do("h_out", [NT, D])
        uT = do("uT", [D, NT], BF16)
        if proj_cols:
            wp = dt("wp", [D, proj_cols])
            p_out = do("p_out", [NT, proj_cols])
    else:
        nF = dt("nF", [128, D])
        out = do("out", [1024, D])
    P = Prog(nc)
    with nc.allow_low_precision("bf16 matmul operands, fp32 accumulation"):
        C = TCtx(P, nc)
        C.load_ident(ident)
        C.load_h(h_in)
        if kind > 0:
            mixer = kind - 1
            if mixer == 1:
                C.outproj(oT, w_out, F_o, colscale=vec_out)
            elif mixer == 3:
                C.outproj(oT, w_out, F_o, bias_dram=vec_out)
            else:
                C.outproj(oT, w_out, F_o)
            C.ffn(nA, wgA, wuA, wdA)
        if kind < 4:
            C.ffn(nB, wgB, wuB, wdB)
            C.norm(nU, 'T')
            C.store_uT(uT)
            C.store_h(h_out)
            if proj_cols:
                C.proj(wp, proj_cols, p_out)
        else:
            C.norm(nF, 'out', out_dram=out)
        P.finish()
    return nc


def _proj(self, w_dram, ncols, p_out):
    P = self.P
    wv = w_dram.rearrange("(kc p) f -> p kc f", p=128)
    CB = 256
    nblk = (ncols + CB - 1) // CB
    ring = self.wg + self.wu
    for b in range(nblk):
        c0 = b * CB
        n = min(CB, ncols - c0)
        wt = ring[b % len(ring)]
        P.op('pool', lambda e, wt=wt, c0=c0, n=n: e.dma_start(out=wt[:, :, 0:n], in_=wv[:, :, c0:c0 + n]), writes=[wt], stream='w')
        for tt, (t0, np_) in enumerate(tok_tiles()):
            py = self.py[self.pyi]
            self.pyi = (self.pyi + 1) % 3
            for kc in range(16):
                P.op('pe', lambda e, py=py, kc=kc, t0=t0, np_=np_, n=n, wt=wt: e.matmul(
                    py[0:np_, 0:n], self.xnT_sb[:, kc, t0:t0 + np_], wt[:, kc, 0:n], start=(kc == 0), stop=(kc == 15)),
                    reads=[self.xnT[tt], wt], writes=[py])
            sg = self.sg[self.sgi]
            self.sgi ^= 1
            P.op('act', lambda e, sg=sg, py=py, np_=np_, n=n: e.copy(sg[0:np_, 0:n], py[0:np_, 0:n]), reads=[py], writes=[sg])
            P.op('sp', lambda e, sg=sg, t0=t0, np_=np_, c0=c0, n=n: e.dma_start(out=p_out[t0:t0 + np_, c0:c0 + n], in_=sg[0:np_, 0:n]),
                 reads=[sg], stream='o')


TCtx.proj = _proj


L = 4112
NCH = 65


def chunk_range(n):
    if n == 0:
        return 0, 16
    return 16 + 64 * (n - 1), 64


def _mk(nc):
    dt = lambda name, shape, dtype=F32: nc.dram_tensor(name, list(shape), dtype, kind="ExternalInput").ap()
    do = lambda name, shape, dtype=F32: nc.dram_tensor(name, list(shape), dtype, kind="ExternalOutput").ap()
    return dt, do


def build_H_pool():
    nc = bass.Bass("TRN2", target_bir_lowering=False)
    dt, do = _mk(nc)
    uc = dt("uc", [512, L], BF16)
    rc16 = dt("rc16", [128, 4, 16])
    pT = do("pT", [512, L], BF16)
    P = Prog(nc)
    O = P.op
    xb = P.tile("xb", [128, 4, L], BF16)
    ob = P.tile("ob", [128, 4, L], BF16)
    A = P.tile("A", [128, 16 + L], F32)
    B = P.tile("B", [128, 16 + L], F32)
    X = P.tile("X", [128, L], F32)
    rc = P.tile("rc", [128, 4, 16], F32)
    t16 = P.tile("t16", [128, 16], F32)
    O('sp', lambda e: e.dma_start(out=xb[:], in_=uc.rearrange("(g p) t -> p g t", p=128)), writes=[xb], stream='a')
    O('sp', lambda e: e.dma_start(out=rc[:], in_=rc16), writes=[rc], stream='a')
    O('dve', lambda e: e.memset(A[:, 0:16], 0.0), writes=[A])
    O('dve', lambda e: e.memset(B[:, 0:16], 0.0), writes=[B])
    for g in range(4):
        w = 2 ** (g + 1)
        O('dve', lambda e, g=g: e.tensor_copy(X[:], xb[:, g, :]), reads=[xb], writes=[X])
        O('dve', lambda e, g=g: e.tensor_copy(A[:, 16:], xb[:, g, :]), reads=[xb], writes=[A])
        cur, oth = A, B
        k = 1
        while k < w:
            O('dve', lambda e, cur=cur, oth=oth, k=k: e.tensor_tensor(oth[:, 16:], cur[:, 16:], cur[:, 16 - k:16 - k + L], ALU.add),
              reads=[cur], writes=[oth])
            cur, oth = oth, cur
            k *= 2
        O('dve', lambda e, cur=cur, g=g, w=w: e.scalar_tensor_tensor(ob[:, g, :], cur[:, 16:], 1.0 / w, X[:], ALU.mult, ALU.subtract),
          reads=[cur, X], writes=[ob])
        O('dve', lambda e, cur=cur, g=g: e.tensor_tensor(t16[:], cur[:, 16:32], rc[:, g, :], ALU.mult), reads=[cur, rc], writes=[t16])
        O('dve', lambda e, g=g: e.tensor_tensor(ob[:, g, 0:16], t16[:], X[:, 0:16], ALU.subtract), reads=[t16, X, ob], writes=[ob])
    O('sp', lambda e: e.dma_start(out=pT.rearrange("(g p) t -> p g t", p=128), in_=ob[:]), reads=[ob], stream='o')
    P.finish()
    return nc


def build_H_swa():
    nc = bass.Bass("TRN2", target_bir_lowering=False)
    dt, do = _mk(nc)
    qT = dt("qT", [512, L]); qsT = dt("qsT", [512, L])
    kT2 = dt("kT2", [128, L]); ksT2 = dt("ksT2", [128, L])
    vtok = dt("vtok", [33 * 128, 64])
    bq = dt("bq", [128, 4]); bqs = dt("bqs", [128, 4]); bk2 = dt("bk2", [128, 1]); bks2 = dt("bks2", [128, 1])
    bv = dt("bv", [128, 64])
    cos2 = dt("cos2", [128, L]); sin2 = dt("sin2", [128, L])
    sinks = dt("sinks", [128, 8])
    mask = dt("mask", [128, 256])
    ident = dt("ident", [128, 128])
    oT = do("oT", [512, L], BF16)
    P = Prog(nc)
    O = P.op
    cosb = P.tile("cosb", [128, L]); sinb = P.tile("sinb", [128, L])
    kr = P.tile("kr", [128, L])
    qr = [P.tile(f"qr{c}", [128, L]) for c in range(4)]
    ta = P.tile("ta", [128, L]); tb = P.tile("tb", [128, L])
    vsb = P.tile("vsb", [128, 33, 64])
    bqt = P.tile("bqt", [128, 4]); bqst = P.tile("bqst", [128, 4]); bkt = P.tile("bkt", [128, 1]); bkst = P.tile("bkst", [128, 1])
    bvt = P.tile("bvt", [128, 64]); sk = P.tile("sk", [128, 8]); mk = P.tile("mk", [128, 256]); idn = P.tile("idn", [128, 128])
    for (t, d) in ((cosb, cos2), (sinb, sin2), (bqt, bq), (bqst, bqs), (bkt, bk2), (bkst, bks2), (bvt, bv), (sk, sinks), (mk, mask), (idn, ident)):
        O('sp', lambda e, t=t, d=d: e.dma_start(out=t[:], in_=d), writes=[t], stream='c')
    O('sp', lambda e: e.dma_start(out=vsb[:], in_=vtok.rearrange("(n p) d -> p n d", p=128)), writes=[vsb], stream='a')
    for i in range(33):
        O('pool', lambda e, i=i: e.tensor_tensor(vsb[:, i, :], vsb[:, i, :], bvt[:], ALU.add), reads=[vsb, bvt], writes=[vsb])

    def rope(dst, src_d, srcs_d, bcol, bscol):
        O('sp', lambda e: e.dma_start(out=ta[:], in_=src_d), writes=[ta], stream='a')
        O('sp', lambda e: e.dma_start(out=tb[:], in_=srcs_d), writes=[tb], stream='a')
        O('dve', lambda e: e.scalar_tensor_tensor(ta[:], ta[:], bcol, cosb[:], ALU.add, ALU.mult), reads=[ta, cosb, bqt, bkt], writes=[ta])
        O('dve', lambda e: e.scalar_tensor_tensor(tb[:], tb[:], bscol, sinb[:], ALU.add, ALU.mult), reads=[tb, sinb, bqst, bkst], writes=[tb])
        O('dve', lambda e: e.tensor_tensor(dst[:], ta[:], tb[:], ALU.add), reads=[ta, tb], writes=[dst])

    rope(kr, kT2, ksT2, bkt[:, 0:1], bkst[:, 0:1])
    for c in range(4):
        rope(qr[c], qT[c * 128:(c + 1) * 128, :], qsT[c * 128:(c + 1) * 128, :], bqt[:, c:c + 1], bqst[:, c:c + 1])

    pss = [P.ptile(f"pss{i}", [128, 256]) for i in range(2)]
    ppt = [P.ptile(f"ppt{i}", [128, 2, 128]) for i in range(2)]
    ppo = [P.ptile(f"ppo{i}", [128, 64]) for i in range(2)]
    pot = P.ptile("pot", [128, 4, 128])
    sm = [P.tile(f"sm{i}", [128, 256]) for i in range(2)]
    pm = [P.tile(f"pm{i}", [128, 256]) for i in range(2)]
    pts = [P.tile(f"pts{i}", [128, 2, 128]) for i in range(2)]
    st = [P.tile(f"st{i}", [128, 8]) for i in range(2)]
    osb = [P.tile(f"osb{i}", [128, 512]) for i in range(2)]
    ots = [P.tile(f"ots{i}", [128, 4, 128], BF16) for i in range(2)]
    u = 0
    for i in range(33):
        np_ = 128 if i < 32 else 16
        q0 = i * 128
        if i == 0:
            k0, nk, m0 = 0, 128, 128
            parts = [(0, 128, 0)]
        else:
            k0, nk, m0 = (i - 1) * 128, 128 + np_, 0
            parts = [(0, 128, i - 1), (128, np_, i)]
        ob_ = osb[i % 2]
        for h in range(8):
            c, base = h // 2, (h % 2) * 64
            ps, smt, pmt, ptp, pt_s, po, s = pss[u % 2], sm[u % 2], pm[u % 2], ppt[u % 2], pts[u % 2], ppo[u % 2], st[u % 2]
            u += 1
            O('pe', lambda e, ps=ps, c=c, base=base: e.matmul(ps[0:np_, 0:nk], qr[c][base:base + 64, q0:q0 + np_], kr[base:base + 64, k0:k0 + nk],
                                                              start=True, stop=True), reads=[qr[c], kr], writes=[ps])
            O('dve', lambda e, ps=ps, smt=smt: e.tensor_tensor(smt[0:np_, 0:nk], ps[0:np_, 0:nk], mk[0:np_, m0:m0 + nk], ALU.add),
              reads=[ps, mk], writes=[smt])
            O('dve', lambda e, smt=smt, s=s: e.reduce_max(s[0:np_, 0:1], smt[0:np_, 0:nk], AX.X), reads=[smt], writes=[s])
            O('dve', lambda e, s=s, h=h: e.tensor_scalar(s[0:np_, 1:2], s[0:np_, 0:1], 0.125, sk[0:np_, h:h + 1], ALU.mult, ALU.max),
              reads=[s, sk], writes=[s])
            O('dve', lambda e, s=s: e.tensor_scalar(s[0:np_, 2:3], s[0:np_, 1:2], -1.0, None, ALU.mult), reads=[s], writes=[s])
            O('act', lambda e, smt=smt, pmt=pmt, s=s: e.activation(pmt[0:np_, 0:nk], smt[0:np_, 0:nk], AF.Exp, bias=s[0:np_, 2:3], scale=0.125,
                                                                    accum_out=s[0:np_, 3:4]), reads=[smt, s], writes=[pmt, s])
            O('act', lambda e, s=s, h=h: e.activation(s[0:np_, 4:5], sk[0:np_, h:h + 1], AF.Exp, bias=s[0:np_, 2:3], scale=1.0),
              reads=[s, sk], writes=[s])
            O('dve', lambda e, s=s: e.tensor_tensor(s[0:np_, 5:6], s[0:np_, 3:4], s[0:np_, 4:5], ALU.add), reads=[s], writes=[s])
            O('dve', lambda e, s=s: e.reciprocal(s[0:np_, 6:7], s[0:np_, 5:6]), reads=[s], writes=[s])
            for j, (co, n, vt) in enumerate(parts):
                O('pe', lambda e, ptp=ptp, pmt=pmt, j=j, co=co, n=n: e.transpose(ptp[0:n, j, 0:np_], pmt[0:np_, co:co + n], idn[0:np_, 0:np_]),
                  reads=[pmt, idn], writes=[ptp])
            O('act', lambda e, ptp=ptp, pt_s=pt_s: e.copy(pt_s[:, :, 0:np_], ptp[:, :, 0:np_]), reads=[ptp], writes=[pt_s])
            for j, (co, n, vt) in enumerate(parts):
                O('pe', lambda e, po=po, pt_s=pt_s, j=j, n=n, vt=vt: e.matmul(po[0:np_, :], pt_s[0:n, j, 0:np_], vsb[0:n, vt, :],
                                                                           start=(j == 0), stop=(j == len(parts) - 1)),
                  reads=[pt_s, vsb], writes=[po])
            O('dve', lambda e, po=po, s=s, h=h: e.tensor_scalar(ob_[0:np_, h * 64:(h + 1) * 64], po[0:np_, :], s[0:np_, 6:7], None, ALU.mult),
              reads=[po, s], writes=[ob_])
        for c in range(4):
            O('pe', lambda e, c=c: e.transpose(pot[:, c, 0:np_], ob_[0:np_, c * 128:(c + 1) * 128], idn[0:np_, 0:np_]),
              reads=[ob_, idn], writes=[pot])
        ot = ots[i % 2]
        O('act', lambda e, ot=ot: e.copy(ot[:, :, 0:np_], pot[:, :, 0:np_]), reads=[pot], writes=[ot])
        O('sp', lambda e, ot=ot: e.dma_start(out=oT.rearrange("(c p) t -> p c t", p=128)[:, :, q0:q0 + np_], in_=ot[:, :, 0:np_]),
          reads=[ot], stream='o')
    P.finish()
    return nc


def build_H_mlstm():
    nc = bass.Bass("TRN2", target_bir_lowering=False)
    dt, do = _mk(nc)
    qT = dt("qT", [2, 128, L]); kT = dt("kT", [2, 128, L])
    kc = dt("kc", [2, NCH * 64, 128]); vext = dt("vext", [2, NCH * 64, 257]); og = dt("og", [2, NCH * 64, 256])
    gi = dt("gi", [2, 1, L]); gf = dt("gf", [2, 1, L])
    bi = dt("bi", [128, 2]); bf = dt("bf", [128, 2])
    nw = dt("nw", [64, 2, 256]); maskT = dt("maskT", [64, 64]); ident = dt("ident", [128, 128])
    oT = do("oT", [512, L], BF16)
    P = Prog(nc)
    O = P.op
    X0 = P.tile("X0", [1, L]); X1 = P.tile("X1", [1, L]); X2 = P.tile("X2", [1, L]); ones = P.tile("ones", [1, L])
    MgE = P.tile("MgE", [1, NCH]); MgP = P.tile("MgP", [1, NCH]); AIa = P.tile("AIa", [1, NCH])
    AI = P.tile("AI", [128, NCH]); COLS = P.tile("COLS", [64, NCH, 3])
    qs = P.tile("qs", [128, L]); kt = P.tile("kt", [128, L])
    bit = P.tile("bit", [128, 2]); bft = P.tile("bft", [128, 2]); nbf = P.tile("nbf", [128, 2]); onec = P.tile("onec", [128, 1])
    nwt = P.tile("nwt", [64, 2, 256]); mT = P.tile("mT", [64, 64]); idn = P.tile("idn", [128, 128])
    Cs = [P.tile(f"C{i}", [128, 257]) for i in range(3)]
    kct = [P.tile(f"kct{i}", [64, 128]) for i in range(3)]
    vt = [P.tile(f"vt{i}", [64, 257]) for i in range(3)]
    ogt = [P.tile(f"ogt{i}", [64, 256]) for i in range(3)]
    STm = [P.tile(f"STm{i}", [64, 64]) for i in range(2)]
    va = [P.tile(f"va{i}", [64, 257]) for i in range(2)]
    sml = [P.tile(f"sml{i}", [64, 8]) for i in range(2)]
    hh = [P.tile(f"hh{i}", [64, 256]) for i in range(2)]
    jk = P.tile("jk", [64, 256])
    sig = [P.tile(f"sig{i}", [64, 256]) for i in range(2)]
    o2 = [P.tile(f"o2{i}", [64, 256]) for i in range(2)]
    oTs = [P.tile(f"oTs{i}", [128, 2, 512], BF16) for i in range(2)]
    pmisc = [P.ptile(f"pmisc{i}", [128, 128]) for i in range(2)]
    psS = [P.ptile(f"psS{i}", [64, 64]) for i in range(2)]
    psN = [P.ptile(f"psN{i}", [64, 257]) for i in range(2)]
    pTr = P.ptile("pTr", [128, 2, 64])
    psC = P.ptile("psC", [128, 257])
    for (t, d) in ((bit, bi), (bft, bf), (nwt, nw), (mT, maskT), (idn, ident)):
        O('sp', lambda e, t=t, d=d: e.dma_start(out=t[:], in_=d), writes=[t], stream='c')
    O('dve', lambda e: e.memset(ones[:], 1.0), writes=[ones])
    O('dve', lambda e: e.memset(onec[:], 1.0), writes=[onec])
    O('dve', lambda e: e.tensor_scalar(nbf[:], bft[:], -1.0, None, ALU.mult), reads=[bft], writes=[nbf])
    oT_v = oT.rearrange("(j p) t -> p j t", p=128)
    mi = 0
    for hd in range(2):
        O('sp', lambda e: e.dma_start(out=X0[:], in_=gi[hd]), writes=[X0], stream='a')
        O('sp', lambda e: e.dma_start(out=X1[:], in_=gf[hd]), writes=[X1], stream='a')
        O('sp', lambda e: e.dma_start(out=qs[:], in_=qT[hd]), writes=[qs], stream='a')
        O('sp', lambda e: e.dma_start(out=kt[:], in_=kT[hd]), writes=[kt], stream='a')
        O('dve', lambda e: e.tensor_scalar(qs[:], qs[:], float(128 ** -0.5), None, ALU.mult), reads=[qs], writes=[qs])
        O('dve', lambda e: e.tensor_scalar(X0[:], X0[:], bit[0:1, hd:hd + 1], None, ALU.add), reads=[X0, bit], writes=[X0])
        O('act', lambda e: e.activation(X1[:], X1[:], AF.Exp, bias=nbf[0:1, hd:hd + 1], scale=-1.0), reads=[X1, nbf], writes=[X1])
        O('act', lambda e: e.activation(X1[:], X1[:], AF.Ln, bias=onec[0:1, 0:1], scale=1.0), reads=[X1, onec], writes=[X1])
        O('dve', lambda e: e.tensor_scalar(X1[:], X1[:], -1.0, None, ALU.mult), reads=[X1], writes=[X1])
        O('dve', lambda e: e.tensor_tensor_scan(X1[:], ones[:], X1[:], 0.0, ALU.mult, ALU.add), reads=[X1, ones], writes=[X1])
        O('dve', lambda e: e.tensor_tensor(X0[:], X0[:], X1[:], ALU.subtract), reads=[X0, X1], writes=[X0])
        O('dve', lambda e: e.tensor_tensor_scan(X2[:], ones[:], X0[:], 0.0, ALU.mult, ALU.max), reads=[X0, ones], writes=[X2])
        O('dve', lambda e: e.tensor_copy(MgE[:, 0:1], X2[:, 15:16]), reads=[X2], writes=[MgE])
        O('dve', lambda e: e.tensor_copy(MgE[:, 1:NCH], X2[0:1, 79:L:64]), reads=[X2], writes=[MgE])
        O('dve', lambda e: e.memset(MgP[:, 0:1], 0.0), writes=[MgP])
        O('dve', lambda e: e.tensor_copy(MgP[:, 1:NCH], MgE[:, 0:NCH - 1]), reads=[MgE], writes=[MgP])
        O('dve', lambda e: e.tensor_tensor(AIa[:], MgP[:], MgE[:], ALU.subtract), reads=[MgP, MgE], writes=[AIa])
        pm_ = pmisc[mi % 2]; mi += 1
        O('pe', lambda e: e.matmul(pm_[:, 0:NCH], ones[0:1, 0:128], AIa[0:1, :], start=True, stop=True), reads=[ones, AIa], writes=[pm_])
        O('act', lambda e: e.activation(AI[:], pm_[:, 0:NCH], AF.Exp), reads=[pm_], writes=[AI])
        O('dve', lambda e: e.scalar_tensor_tensor(X1[:], X1[:], -1.0, X2[:], ALU.mult, ALU.subtract), reads=[X1, X2], writes=[X1])
        for n in range(NCH):
            c0, C = chunk_range(n)
            O('dve', lambda e: e.tensor_scalar(X0[:, c0:c0 + C], X0[:, c0:c0 + C], MgE[:, n:n + 1], None, ALU.subtract), reads=[X0, MgE], writes=[X0])
            O('dve', lambda e: e.tensor_scalar(X2[:, c0:c0 + C], X2[:, c0:c0 + C], -1.0, MgE[:, n:n + 1], ALU.mult, ALU.add), reads=[X2, MgE], writes=[X2])
        for X in (X0, X1, X2):
            O('act', lambda e: e.activation(X[:], X[:], AF.Exp), reads=[X], writes=[X])
        for n in range(NCH):
            c0, C = chunk_range(n)
            pm_ = pmisc[mi % 2]; mi += 1
            for j, X in enumerate((X0, X2, X1)):
                O('pe', lambda e: e.matmul(pm_[0:C, j:j + 1], X[0:1, c0:c0 + C], ones[0:1, 0:1], start=True, stop=True), reads=[X, ones], writes=[pm_])
            O('act', lambda e: e.copy(COLS[0:C, n, :], pm_[0:C, 0:3]), reads=[pm_], writes=[COLS])
        O('dve', lambda e: e.memset(Cs[0][:], 0.0), writes=[Cs[0]])
        ci = 0
        for n in range(NCH):
            c0, C = chunk_range(n)
            r3 = n % 3; r2 = n % 2
            k_, v_, g_ = kct[r3], vt[r3], ogt[r3]
            O('sp', lambda e: e.dma_start(out=k_[:], in_=kc[hd, n * 64:(n + 1) * 64, :]), writes=[k_], stream='a')
            O('sp', lambda e: e.dma_start(out=v_[:], in_=vext[hd, n * 64:(n + 1) * 64, :]), writes=[v_], stream='a')
            O('sp', lambda e: e.dma_start(out=g_[:], in_=og[hd, n * 64:(n + 1) * 64, :]), writes=[g_], stream='a')
            Cc, Cp, Cn = Cs[ci % 3], Cs[(ci + 1) % 3], Cs[(ci + 2) % 3]
            ci += 2
            O('dve', lambda e: e.tensor_scalar(Cp[:], Cc[:], AI[:, n:n + 1], None, ALU.mult), reads=[Cc, AI], writes=[Cp])
            pS, pN, st_, va_, sm_, hh_, sg_, o2_ = psS[r2], psN[r2], STm[r2], va[r2], sml[r2], hh[r2], sig[r2], o2[r2]
            O('pe', lambda e: e.matmul(pS[0:C, 0:C], kt[:, c0:c0 + C], qs[:, c0:c0 + C], start=True, stop=True), reads=[kt, qs], writes=[pS])
            O('dve', lambda e: e.tensor_tensor(st_[0:C, 0:C], pS[0:C, 0:C], mT[0:C, 0:C], ALU.mult), reads=[pS, mT], writes=[st_])
            O('dve', lambda e: e.tensor_scalar(va_[0:C, :], v_[0:C, :], COLS[0:C, n, 0:1], None, ALU.mult), reads=[v_, COLS], writes=[va_])
            O('pe', lambda e: e.matmul(pN[0:C, :], qs[:, c0:c0 + C], Cp[:, :], start=True, stop=False), reads=[qs, Cp], writes=[pN])
            O('pe', lambda e: e.matmul(pN[0:C, :], st_[0:C, 0:C], va_[0:C, :], start=False, stop=True), reads=[st_, va_], writes=[pN])
            O('dve', lambda e: e.tensor_tensor(sm_[0:C, 0:1], pN[0:C, 256:257], COLS[0:C, n, 1:2], ALU.mult), reads=[pN, COLS], writes=[sm_])
            O('dve', lambda e: e.tensor_scalar(sm_[0:C, 7:8], sm_[0:C, 0:1], -1.0, None, ALU.mult), reads=[sm_], writes=[sm_])
            O('dve', lambda e: e.tensor_tensor(sm_[0:C, 0:1], sm_[0:C, 0:1], sm_[0:C, 7:8], ALU.max), reads=[sm_], writes=[sm_])
            O('dve', lambda e: e.tensor_tensor(sm_[0:C, 1:2], sm_[0:C, 0:1], COLS[0:C, n, 2:3], ALU.max), reads=[sm_, COLS], writes=[sm_])
            O('dve', lambda e: e.reciprocal(sm_[0:C, 7:8], sm_[0:C, 1:2]), reads=[sm_], writes=[sm_])
            O('dve', lambda e: e.tensor_tensor(sm_[0:C, 2:3], COLS[0:C, n, 1:2], sm_[0:C, 7:8], ALU.mult), reads=[sm_, COLS], writes=[sm_])
            O('dve', lambda e: e.tensor_scalar(hh_[0:C, :], pN[0:C, 0:256], sm_[0:C, 2:3], None, ALU.mult), reads=[pN, sm_], writes=[hh_])
            O('act', lambda e: e.activation(jk[0:C, :], hh_[0:C, :], AF.Square, accum_out=sm_[0:C, 3:4]), reads=[hh_], writes=[jk, sm_])
            O('dve', lambda e: e.tensor_scalar(sm_[0:C, 4:5], sm_[0:C, 3:4], 1.0 / 256, 1e-6, ALU.mult, ALU.add), reads=[sm_], writes=[sm_])
            O('act', lambda e: e.activation(sm_[0:C, 5:6], sm_[0:C, 4:5], AF.Sqrt), reads=[sm_], writes=[sm_])
            O('dve', lambda e: e.reciprocal(sm_[0:C, 6:7], sm_[0:C, 5:6]), reads=[sm_], writes=[sm_])
            O('act', lambda e: e.activation(sg_[0:C, :], g_[0:C, :], AF.Sigmoid), reads=[g_], writes=[sg_])
            O('dve', lambda e: e.scalar_tensor_tensor(o2_[0:C, :], hh_[0:C, :], sm_[0:C, 6:7], nwt[0:C, hd, :], ALU.mult, ALU.mult),
              reads=[hh_, sm_, nwt], writes=[o2_])
            O('dve', lambda e: e.tensor_tensor(o2_[0:C, :], o2_[0:C, :], sg_[0:C, :], ALU.mult), reads=[o2_, sg_], writes=[o2_])
            for j in range(2):
                O('pe', lambda e: e.transpose(pTr[:, j, 0:C], o2_[0:C, j * 128:(j + 1) * 128], idn[0:C, 0:C]), reads=[o2_, idn], writes=[pTr])
            if n == 0:
                ot, off, gstart, glen, last = oTs[0], 0, 0, 16, True
            else:
                gidx = (n - 1) // 8
                ot, off, gstart, glen, last = oTs[(gidx + 1) % 2], ((n - 1) % 8) * 64, 16 + gidx * 512, 512, (n - 1) % 8 == 7
            O('act', lambda e: e.copy(ot[:, :, off:off + C], pTr[:, :, 0:C]), reads=[pTr], writes=[ot])
            if last:
                O('sp', lambda e: e.dma_start(out=oT_v[:, 2 * hd:2 * hd + 2, gstart:gstart + glen], in_=ot[:, :, 0:glen]), reads=[ot], stream='o')
            O('pe', lambda e: e.matmul(psC[:, :], k_[0:C, :], va_[0:C, :], start=True, stop=True), reads=[k_, va_], writes=[psC])
            O('dve', lambda e: e.tensor_tensor(Cn[:], Cp[:], psC[:], ALU.add), reads=[Cp, psC], writes=[Cn])
    P.finish()
    return nc


def build_H_gdn():
    nc = bass.Bass("TRN2", target_bir_lowering=False)
    dt, do = _mk(nc)
    xqk = dt("xqk", [8, 128, 3 + L]); xv = dt("xv", [8, 128, 3 + L])
    cw = dt("cw", [128, 16, 4])
    zc = dt("zc", [8, NCH * 64, 128])
    bpre = dt("bpre", [64, 8, NCH]); apre = dt("apre", [64, 8, NCH])
    alog = dt("alog", [64, 8, NCH]); dtb = dt("dtb", [64, 8, NCH])
    nw = dt("nw", [64, 128]); mneg = dt("mneg", [64, 64]); mlt = dt("mlt", [64, 64]); ident = dt("ident", [128, 128])
    utri = dt("utri", [64, 64]); sel63 = dt("sel63", [64, 128]); sel15 = dt("sel15", [64, 128])
    oT = do("oT", [1024, L], BF16)
    P = Prog(nc)
    O = P.op
    NT8 = 8 * NCH
    cwt = P.tile("cwt", [128, 16, 4]); nwt = P.tile("nwt", [64, 128]); mng = P.tile("mng", [64, 64]); mltt = P.tile("mltt", [64, 64])
    idn = P.tile("idn", [128, 128]); utr = P.tile("utr", [64, 64]); s63 = P.tile("s63", [64, 128]); s15 = P.tile("s15", [64, 128])
    ones = P.tile("ones", [128, 128])
    for (t, d) in ((cwt, cw), (nwt, nw), (mng, mneg), (mltt, mlt), (idn, ident), (utr, utri), (s63, sel63), (s15, sel15)):
        O('sp', lambda e: e.dma_start(out=t[:], in_=d), writes=[t], stream='c')
    O('dve', lambda e: e.memset(ones[:], 1.0), writes=[ones])
    tb = {k: P.tile("tb_" + k, [64, NT8]) for k in ("x", "ax", "l", "g", "beta", "nbeta", "gc", "egc", "eks", "beg", "nea")}
    GLb = P.tile("GLb", [128, NT8]); EGL = P.tile("EGL", [128, NT8])
    pbig = P.ptile("pbig", [128, 512])
    pps = [P.ptile(f"pp{i}", [128, 128]) for i in range(7)]
    pstate = {'i': 0}

    def ps():
        t = pps[pstate['i'] % 7]
        pstate['i'] += 1
        return t

    fl = lambda d: d.rearrange("p h n -> p (h n)")
    O('sp', lambda e: e.dma_start(out=tb["x"][:], in_=fl(apre)), writes=[tb["x"]], stream='a')
    O('sp', lambda e: e.dma_start(out=tb["l"][:], in_=fl(dtb)), writes=[tb["l"]], stream='a')
    O('sp', lambda e: e.dma_start(out=tb["nea"][:], in_=fl(alog)), writes=[tb["nea"]], stream='a')
    O('sp', lambda e: e.dma_start(out=tb["beta"][:], in_=fl(bpre)), writes=[tb["beta"]], stream='a')
    x, ax, l, g = tb["x"], tb["ax"], tb["l"], tb["g"]
    O('dve', lambda e: e.tensor_tensor(x[:], x[:], l[:], ALU.add), reads=[x, l], writes=[x])
    O('dve', lambda e: e.tensor_scalar(ax[:], x[:], -1.0, None, ALU.mult), reads=[x], writes=[ax])
    O('dve', lambda e: e.tensor_tensor(ax[:], ax[:], x[:], ALU.max), reads=[ax, x], writes=[ax])
    O('act', lambda e: e.activation(l[:], ax[:], AF.Exp, scale=-1.0), reads=[ax], writes=[l])
    O('dve', lambda e: e.tensor_scalar(l[:], l[:], 1.0, None, ALU.add), reads=[l], writes=[l])
    O('act', lambda e: e.activation(l[:], l[:], AF.Ln), reads=[l], writes=[l])
    O('dve', lambda e: e.tensor_scalar(ax[:], x[:], 0.0, None, ALU.max), reads=[x], writes=[ax])
    O('dve', lambda e: e.tensor_tensor(l[:], l[:], ax[:], ALU.add), reads=[l, ax], writes=[l])
    O('act', lambda e: e.activation(tb["nea"][:], tb["nea"][:], AF.Exp), reads=[tb["nea"]], writes=[tb["nea"]])
    O('dve', lambda e: e.scalar_tensor_tensor(g[:], l[:], -1.0, tb["nea"][:], ALU.mult, ALU.mult), reads=[l, tb["nea"]], writes=[g])
    O('act', lambda e: e.activation(tb["beta"][:], tb["beta"][:], AF.Sigmoid), reads=[tb["beta"]], writes=[tb["beta"]])
    O('dve', lambda e: e.tensor_scalar(tb["nbeta"][:], tb["beta"][:], -1.0, None, ALU.mult), reads=[tb["beta"]], writes=[tb["nbeta"]])
    for half in range(2):
        cs = slice(half * 4 * NCH, (half + 1) * 4 * NCH)
        O('pe', lambda e: e.matmul(pbig[0:64, 0:4 * NCH], utr[:, :], g[:, cs], start=True, stop=True), reads=[utr, g], writes=[pbig])
        O('act', lambda e: e.copy(tb["gc"][:, cs], pbig[0:64, 0:4 * NCH]), reads=[pbig], writes=[tb["gc"]])
    gc = tb["gc"]
    for half in range(2):
        cs = slice(half * 4 * NCH, (half + 1) * 4 * NCH)
        O('pe', lambda e: e.matmul(pbig[:, 0:4 * NCH], s63[:, :], gc[:, cs], start=True, stop=True), reads=[s63, gc], writes=[pbig])
        O('act', lambda e: e.copy(GLb[:, cs], pbig[:, 0:4 * NCH]), reads=[pbig], writes=[GLb])
    for half in range(2):
        cs = slice(half * 4 * NCH, (half + 1) * 4 * NCH)
        O('pe', lambda e: e.matmul(pbig[:, 0:4 * NCH], s15[:, :], gc[:, cs], start=True, stop=True), reads=[s15, gc], writes=[pbig])
        for hh_ in range(4):
            col = (half * 4 + hh_) * NCH
            O('act', lambda e: e.copy(GLb[:, col:col + 1], pbig[:, hh_ * NCH:hh_ * NCH + 1]), reads=[pbig], writes=[GLb])
    O('act', lambda e: e.activation(EGL[:], GLb[:], AF.Exp), reads=[GLb], writes=[EGL])
    O('act', lambda e: e.activation(tb["egc"][:], gc[:], AF.Exp), reads=[gc], writes=[tb["egc"]])
    O('dve', lambda e: e.tensor_tensor(tb["eks"][:], GLb[0:64, :], gc[:], ALU.subtract), reads=[GLb, gc], writes=[tb["eks"]])
    O('act', lambda e: e.activation(tb["eks"][:], tb["eks"][:], AF.Exp), reads=[tb["eks"]], writes=[tb["eks"]])
    O('dve', lambda e: e.tensor_tensor(tb["beg"][:], tb["beta"][:], tb["egc"][:], ALU.mult), reads=[tb["beta"], tb["egc"]], writes=[tb["beg"]])

    xin = [P.tile(f"xin{i}", [128, 3 + L]) for i in range(2)]
    QN = P.tile("QN", [128, L]); KN = P.tile("KN", [128, L]); V = [P.tile(f"V{i}", [128, L]) for i in range(2)]
    sq = P.tile("sq", [128, 512]); rn = P.tile("rn", [128, 512])
    xi = {'i': 0}

    def conv(dst, src_dram, ch):
        xt = xin[xi['i'] % 2]
        xi['i'] += 1
        O('sp', lambda e: e.dma_start(out=xt[:], in_=src_dram), writes=[xt], stream='a')
        O('dve', lambda e: e.tensor_scalar(dst[:], xt[:, 3:3 + L], cwt[:, ch, 3:4], None, ALU.mult), reads=[xt, cwt], writes=[dst])
        for j in (2, 1, 0):
            O('dve', lambda e: e.scalar_tensor_tensor(dst[:], xt[:, j:j + L], cwt[:, ch, j:j + 1], dst[:], ALU.mult, ALU.add),
              reads=[xt, cwt, dst], writes=[dst])
        O('act', lambda e: e.activation(dst[:], dst[:], AF.Silu), reads=[dst], writes=[dst])

    def l2n(dst, scale):
        for b0 in range(0, L, 512):
            n = min(512, L - b0)
            O('act', lambda e: e.activation(sq[:, 0:n], dst[:, b0:b0 + n], AF.Square), reads=[dst], writes=[sq])
            O('pe', lambda e: e.matmul(pbig[:, 0:n], ones[:, :], sq[:, 0:n], start=True, stop=True), reads=[ones, sq], writes=[pbig])
            O('dve', lambda e: e.tensor_scalar(rn[:, 0:n], pbig[:, 0:n], 1e-6, None, ALU.add), reads=[pbig], writes=[rn])
            O('act', lambda e: e.activation(rn[:, 0:n], rn[:, 0:n], AF.Sqrt), reads=[rn], writes=[rn])
            O('dve', lambda e: e.reciprocal(rn[:, 0:n], rn[:, 0:n]), reads=[rn], writes=[rn])
            O('dve', lambda e: e.scalar_tensor_tensor(dst[:, b0:b0 + n], dst[:, b0:b0 + n], float(scale), rn[:, 0:n], ALU.mult, ALU.mult),
              reads=[dst, rn], writes=[dst])

    scr = {}

    def sc(name, par, shape=(128, 128)):
        key = (name, par)
        if key not in scr:
            scr[key] = P.tile(f"s_{name}_{par}", list(shape))
        return scr[key]

    S = [[P.tile(f"S{h}_{i}", [128, 128]) for i in range(2)] for h in range(2)]
    oTs = [[P.tile(f"oTs{h}_{i}", [128, 512], BF16) for i in range(2)] for h in range(2)]
    zt = [[P.tile(f"zt{h}_{i}", [64, 128]) for i in range(2)] for h in range(2)]

    for jq in range(4):
        conv(QN, xqk[jq], jq)
        l2n(QN, 128 ** -0.5)
        conv(KN, xqk[4 + jq], 4 + jq)
        l2n(KN, 1.0)
        for hl in range(2):
            conv(V[hl], xv[2 * jq + hl], 8 + 2 * jq + hl)
            O('dve', lambda e: e.memset(S[hl][0][:], 0.0), writes=[S[hl][0]])
        for n in range(NCH):
            c0, C = chunk_range(n)
            kTc = lambda: KN[:, c0:c0 + C]
            qTc = lambda: QN[:, c0:c0 + C]
            p1 = ps(); ktok = sc("ktok", 0)
            O('pe', lambda e: e.transpose(p1[0:C, :], KN[:, c0:c0 + C], idn[:, :]), reads=[KN, idn], writes=[p1])
            O('act', lambda e: e.copy(ktok[0:C, :], p1[0:C, :]), reads=[p1], writes=[ktok])
            p2 = ps(); KKs = sc("KKs", 0)
            O('pe', lambda e: e.matmul(p2[0:C, 0:C], KN[:, c0:c0 + C], KN[:, c0:c0 + C], start=True, stop=True), reads=[KN], writes=[p2])
            O('act', lambda e: e.copy(KKs[0:C, 0:C], p2[0:C, 0:C]), reads=[p2], writes=[KKs])
            p3 = ps(); QKs = sc("QKs", 0)
            O('pe', lambda e: e.matmul(p3[0:C, 0:C], QN[:, c0:c0 + C], KN[:, c0:c0 + C], start=True, stop=True), reads=[QN, KN], writes=[p3])
            O('act', lambda e: e.copy(QKs[0:C, 0:C], p3[0:C, 0:C]), reads=[p3], writes=[QKs])
            for hl in range(2):
                hv = 2 * jq + hl
                col = hv * NCH + n
                cc = freeze(lambda t: t[0:C, col:col + 1])
                Sc, Sn = S[hl][n % 2], S[hl][(n + 1) % 2]
                z_ = zt[hl][n % 2]
                O('sp', lambda e: e.dma_start(out=z_[:], in_=zc[hv, n * 64:(n + 1) * 64, :]), writes=[z_], stream='a')
                p = ps(); vtok = sc("vtok", hl)
                O('pe', lambda e: e.transpose(p[0:C, :], V[hl][:, c0:c0 + C], idn[:, :]), reads=[V[hl], idn], writes=[p])
                O('act', lambda e: e.copy(vtok[0:C, :], p[0:C, :]), reads=[p], writes=[vtok])
                dg = sc("dg", hl)
                O('dve', lambda e: e.tensor_scalar(dg[0:C, 0:C], idn[0:C, 0:C], cc(gc), None, ALU.mult), reads=[idn, gc], writes=[dg])
                p = ps(); Xm = sc("Xm", hl); Dm = sc("Dm", hl)
                O('pe', lambda e: e.matmul(p[0:C, 0:C], ones[0:C, 0:C], dg[0:C, 0:C], start=True, stop=True), reads=[ones, dg], writes=[p])
                O('dve', lambda e: e.scalar_tensor_tensor(Xm[0:C, 0:C], p[0:C, 0:C], -1.0, mng[0:C, 0:C], ALU.mult, ALU.add), reads=[p, mng], writes=[Xm])
                O('act', lambda e: e.activation(Dm[0:C, 0:C], Xm[0:C, 0:C], AF.Exp, bias=cc(gc), scale=1.0), reads=[Xm, gc], writes=[Dm])
                Pk = sc("P0", hl); Qk = sc("Q0", hl)
                O('dve', lambda e: e.scalar_tensor_tensor(Pk[0:C, 0:C], KKs[0:C, 0:C], cc(tb["nbeta"]), Dm[0:C, 0:C], ALU.mult, ALU.mult),
                  reads=[KKs, tb["nbeta"], Dm], writes=[Pk])
                O('dve', lambda e: e.tensor_tensor(Pk[0:C, 0:C], Pk[0:C, 0:C], mltt[0:C, 0:C], ALU.mult), reads=[Pk, mltt], writes=[Pk])
                at = sc("attn", hl); atT = sc("attnT", hl)
                O('dve', lambda e: e.tensor_tensor(at[0:C, 0:C], QKs[0:C, 0:C], Dm[0:C, 0:C], ALU.mult), reads=[QKs, Dm], writes=[at])
                p = ps()
                O('pe', lambda e: e.transpose(p[0:C, 0:C], at[0:C, 0:C], idn[0:C, 0:C]), reads=[at, idn], writes=[p])
                O('act', lambda e: e.copy(atT[0:C, 0:C], p[0:C, 0:C]), reads=[p], writes=[atT])
                p = ps()
                O('pe', lambda e: e.transpose(p[0:C, 0:C], Pk[0:C, 0:C], idn[0:C, 0:C]), reads=[Pk, idn], writes=[p])
                O('act', lambda e: e.copy(Qk[0:C, 0:C], p[0:C, 0:C]), reads=[p], writes=[Qk])
                RT = sc("RT0", hl)
                O('dve', lambda e: e.tensor_tensor(RT[0:C, 0:C], idn[0:C, 0:C], Qk[0:C, 0:C], ALU.add), reads=[idn, Qk], writes=[RT])
                for k in range(5):
                    Pn = sc(f"P{k + 1}", hl)
                    p = ps()
                    O('pe', lambda e: e.matmul(p[0:C, 0:C], Qk[0:C, 0:C], Pk[0:C, 0:C], start=True, stop=True), reads=[Qk, Pk], writes=[p])
                    O('act', lambda e: e.copy(Pn[0:C, 0:C], p[0:C, 0:C]), reads=[p], writes=[Pn])
                    if k < 4:
                        Qn = sc(f"Q{k + 1}", hl)
                        p = ps()
                        O('pe', lambda e: e.matmul(p[0:C, 0:C], Pk[0:C, 0:C], Qk[0:C, 0:C], start=True, stop=True), reads=[Qk, Pk], writes=[p])
                        O('dve', lambda e: e.tensor_copy(Qn[0:C, 0:C], p[0:C, 0:C]), reads=[p], writes=[Qn])
                        Qk = Qn
                    Pk = Pn
                    RTn = sc(f"RT{k + 1}", hl)
                    p = ps()
                    O('pe', lambda e: e.matmul(p[0:C, 0:C], Pk[0:C, 0:C], RT[0:C, 0:C], start=True, stop=True), reads=[Pk, RT], writes=[p])
                    O('dve', lambda e: e.tensor_tensor(RTn[0:C, 0:C], RT[0:C, 0:C], p[0:C, 0:C], ALU.add), reads=[RT, p], writes=[RTn])
                    RT = RTn
                Tm = RT
                vb = sc("vb", hl); kbg = sc("kbg", hl); ksc = sc("ksc", hl)
                O('dve', lambda e: e.tensor_scalar(vb[0:C, :], vtok[0:C, :], cc(tb["beta"]), None, ALU.mult), reads=[vtok, tb["beta"]], writes=[vb])
                O('dve', lambda e: e.tensor_scalar(kbg[0:C, :], ktok[0:C, :], cc(tb["beg"]), None, ALU.mult), reads=[ktok, tb["beg"]], writes=[kbg])
                O('dve', lambda e: e.tensor_scalar(ksc[0:C, :], ktok[0:C, :], cc(tb["eks"]), None, ALU.mult), reads=[ktok, tb["eks"]], writes=[ksc])
                p = ps(); nwT = sc("nwT", hl)
                O('pe', lambda e: e.matmul(p[:, 0:C], kbg[0:C, :], Tm[0:C, 0:C], start=True, stop=True), reads=[kbg, Tm], writes=[p])
                O('dve', lambda e: e.tensor_scalar(nwT[:, 0:C], p[:, 0:C], -1.0, None, ALU.mult), reads=[p], writes=[nwT])
                p = ps(); vnew = sc("vnew", hl)
                O('pe', lambda e: e.matmul(p[0:C, :], Tm[0:C, 0:C], vb[0:C, :], start=True, stop=False), reads=[Tm, vb], writes=[p])
                O('pe', lambda e: e.matmul(p[0:C, :], nwT[:, 0:C], Sc[:, :], start=False, stop=True), reads=[nwT, Sc], writes=[p])
                O('act', lambda e: e.copy(vnew[0:C, :], p[0:C, :]), reads=[p], writes=[vnew])
                pa = ps(); oa = sc("oa", hl)
                O('pe', lambda e: e.matmul(pa[0:C, :], atT[0:C, 0:C], vnew[0:C, :], start=True, stop=True), reads=[atT, vnew], writes=[pa])
                O('act', lambda e: e.copy(oa[0:C, :], pa[0:C, :]), reads=[pa], writes=[oa])
                pi = ps(); oo = sc("oo", hl)
                O('pe', lambda e: e.matmul(pi[0:C, :], QN[:, c0:c0 + C], Sc[:, :], start=True, stop=True), reads=[QN, Sc], writes=[pi])
                O('dve', lambda e: e.scalar_tensor_tensor(oo[0:C, :], pi[0:C, :], cc(tb["egc"]), oa[0:C, :], ALU.mult, ALU.add),
                  reads=[pi, tb["egc"], oa], writes=[oo])
                p = ps()
                O('pe', lambda e: e.matmul(p[:, :], ksc[0:C, :], vnew[0:C, :], start=True, stop=True), reads=[ksc, vnew], writes=[p])
                O('dve', lambda e: e.scalar_tensor_tensor(Sn[:, :], Sc[:, :], EGL[:, col:col + 1], p[:, :], ALU.mult, ALU.add),
                  reads=[Sc, EGL, p], writes=[Sn])
                sm_ = sc("sm", hl, (64, 8)); jk = sc("jk", hl); zs = sc("zs", hl); o2_ = sc("o2", hl)
                O('act', lambda e: e.activation(jk[0:C, :], oo[0:C, :], AF.Square, accum_out=sm_[0:C, 0:1]), reads=[oo], writes=[jk, sm_])
                O('dve', lambda e: e.tensor_scalar(sm_[0:C, 1:2], sm_[0:C, 0:1], 1.0 / 128, 1e-6, ALU.mult, ALU.add), reads=[sm_], writes=[sm_])
                O('act', lambda e: e.activation(sm_[0:C, 2:3], sm_[0:C, 1:2], AF.Sqrt), reads=[sm_], writes=[sm_])
                O('dve', lambda e: e.reciprocal(sm_[0:C, 3:4], sm_[0:C, 2:3]), reads=[sm_], writes=[sm_])
                O('act', lambda e: e.activation(zs[0:C, :], z_[0:C, :], AF.Silu), reads=[z_], writes=[zs])
                O('dve', lambda e: e.scalar_tensor_tensor(o2_[0:C, :], oo[0:C, :], sm_[0:C, 3:4], nwt[0:C, :], ALU.mult, ALU.mult),
                  reads=[oo, sm_, nwt], writes=[o2_])
                O('dve', lambda e: e.tensor_tensor(o2_[0:C, :], o2_[0:C, :], zs[0:C, :], ALU.mult), reads=[o2_, zs], writes=[o2_])
                p = ps()
                O('pe', lambda e: e.transpose(p[:, 0:C], o2_[0:C, :], idn[0:C, 0:C]), reads=[o2_, idn], writes=[p])
                if n == 0:
                    ot, off, gstart, glen, last = oTs[hl][0], 0, 0, 16, True
                else:
                    gidx = (n - 1) // 8
                    ot, off, gstart, glen, last = oTs[hl][(gidx + 1) % 2], ((n - 1) % 8) * 64, 16 + gidx * 512, 512, (n - 1) % 8 == 7
                O('act', lambda e: e.copy(ot[:, off:off + C], p[:, 0:C]), reads=[p], writes=[ot])
                if last:
                    O('sp', lambda e: e.dma_start(out=oT[hv * 128:(hv + 1) * 128, gstart:gstart + glen], in_=ot[:, 0:glen]), reads=[ot], stream='o')
    P.finish()
    return nc


import numpy as np
L = 4112


def rope_tables():
    inv = (np.float32(10000.0) ** (-(np.arange(0, 64, 2, dtype=np.float32)) / np.float32(64))).astype(np.float32)
    ang = (np.arange(L, dtype=np.float32)[:, None] * inv[None, :]).astype(np.float32)
    ang = np.concatenate([ang, ang], axis=-1)
    cos = np.cos(ang).astype(np.float32)
    sin = np.sin(ang).astype(np.float32)
    sgn = np.concatenate([-np.ones(32, np.float32), np.ones(32, np.float32)])
    return cos, sin * sgn[None, :]


def swa_consts():
    cos, sins = rope_tables()
    cos2 = np.ascontiguousarray(np.concatenate([cos.T, cos.T], axis=0))
    sin2 = np.ascontiguousarray(np.concatenate([sins.T, sins.T], axis=0))
    q = np.arange(128)[:, None]
    j = np.arange(128)[None, :]
    prev = np.where(j > q, 0.0, -30000.0)
    cur = np.where(j <= q, 0.0, -30000.0)
    mask = np.concatenate([prev, cur], axis=1).astype(np.float32)
    return cos2, sin2, mask


def swap_halves_cols(a):
    sh = a.shape
    a = a.reshape(sh[:-1] + (sh[-1] // 64, 2, 32))
    a = a[..., ::-1, :]
    return np.ascontiguousarray(a.reshape(sh))


def swa_inputs(p_b, b_qkv, sinks, kvh, consts):
    cos2, sin2, mask = consts
    qc = slice(kvh * 512, (kvh + 1) * 512)
    kc = slice(2048 + kvh * 64, 2048 + (kvh + 1) * 64)
    vc = slice(2048 + 256 + kvh * 64, 2048 + 256 + (kvh + 1) * 64)
    q = p_b[:, qc]; k = p_b[:, kc]; v = p_b[:, vc]
    bq = b_qkv[qc]; bk = b_qkv[kc]; bv = b_qkv[vc]
    d = {}
    d["qT"] = np.ascontiguousarray(q.T)
    d["qsT"] = np.ascontiguousarray(swap_halves_cols(q).T)
    d["kT2"] = np.ascontiguousarray(np.concatenate([k.T, k.T], axis=0))
    ks = swap_halves_cols(k)
    d["ksT2"] = np.ascontiguousarray(np.concatenate([ks.T, ks.T], axis=0))
    vp = np.zeros((33 * 128, 64), np.float32); vp[:L] = v
    d["vtok"] = vp
    d["bq"] = np.ascontiguousarray(bq.reshape(4, 128).T)
    d["bqs"] = np.ascontiguousarray(swap_halves_cols(bq).reshape(4, 128).T)
    d["bk2"] = np.ascontiguousarray(np.concatenate([bk, bk])[:, None])
    bks = swap_halves_cols(bk)
    d["bks2"] = np.ascontiguousarray(np.concatenate([bks, bks])[:, None])
    d["bv"] = np.ascontiguousarray(np.broadcast_to(bv[None, :], (128, 64)))
    d["cos2"] = cos2; d["sin2"] = sin2; d["mask"] = mask
    d["sinks"] = np.ascontiguousarray(np.broadcast_to(sinks[kvh * 8:(kvh + 1) * 8][None, :], (128, 8)))
    d["ident"] = np.eye(128, dtype=np.float32)
    return d


def chunked(a):
    out = np.zeros((65 * 64, a.shape[1]), np.float32)
    out[0:16] = a[0:16]
    out[64:] = a[16:]
    return out


def mlstm_inputs(p_b, b_if, norm_w, hp):
    d = {k: [] for k in ("qT", "kT", "kc", "vext", "og", "gi", "gf")}
    for hd in range(2):
        h = 2 * hp + hd
        q = p_b[:, h * 128:(h + 1) * 128]; k = p_b[:, 1024 + h * 128:1024 + (h + 1) * 128]
        v = p_b[:, 2048 + h * 256:2048 + (h + 1) * 256]; og = p_b[:, 4096 + h * 256:4096 + (h + 1) * 256]
        d["qT"].append(q.T); d["kT"].append(k.T); d["kc"].append(chunked(k))
        d["vext"].append(chunked(np.concatenate([v, np.ones((L, 1), np.float32)], axis=1)))
        d["og"].append(chunked(og))
        d["gi"].append(p_b[:, 6144 + h][None, :]); d["gf"].append(p_b[:, 6144 + 8 + h][None, :])
    d = {k: np.ascontiguousarray(np.stack(v)).astype(np.float32) for k, v in d.items()}
    hs = [2 * hp, 2 * hp + 1]
    d["bi"] = np.ascontiguousarray(np.broadcast_to(b_if[hs][None, :], (128, 2))).astype(np.float32)
    d["bf"] = np.ascontiguousarray(np.broadcast_to(b_if[[8 + x for x in hs]][None, :], (128, 2))).astype(np.float32)
    nwh = np.stack([norm_w[h * 256:(h + 1) * 256] for h in hs])
    d["nw"] = np.ascontiguousarray(np.broadcast_to(nwh[None], (64, 2, 256))).astype(np.float32)
    s = np.arange(64)[:, None]; t = np.arange(64)[None, :]
    d["maskT"] = (s <= t).astype(np.float32)
    d["ident"] = np.eye(128, dtype=np.float32)
    return d


def chunk_cols(a):
    return np.ascontiguousarray(chunked(a[:, None])[:, 0].reshape(65, 64).T)


def pad3T(a):
    out = np.zeros((a.shape[1], 3 + L), np.float32)
    out[:, 3:] = a.T
    return out


def gdn_consts():
    t = np.arange(64)[:, None]; s = np.arange(64)[None, :]
    d = {}
    d["mneg"] = np.where(s <= t, 0.0, -30000.0).astype(np.float32)
    d["mlt"] = (s < t).astype(np.float32)
    d["utri"] = (t <= s).astype(np.float32)
    d["sel63"] = np.zeros((64, 128), np.float32); d["sel63"][63] = 1.0
    d["sel15"] = np.zeros((64, 128), np.float32); d["sel15"][15] = 1.0
    d["ident"] = np.eye(128, dtype=np.float32)
    return d


def gdn_inputs(p_b, conv_w, a_log, dt_bias, norm_w, j4, consts):
    d = dict(consts)
    hqs = [4 * j4 + i for i in range(4)]; hvs = [8 * j4 + i for i in range(8)]
    d["xqk"] = np.stack([pad3T(p_b[:, h * 128:(h + 1) * 128]) for h in hqs] + [pad3T(p_b[:, 2048 + h * 128:2048 + (h + 1) * 128]) for h in hqs])
    d["xv"] = np.stack([pad3T(p_b[:, 4096 + h * 128:4096 + (h + 1) * 128]) for h in hvs])
    chans = [np.arange(h * 128, (h + 1) * 128) for h in hqs] + [2048 + np.arange(h * 128, (h + 1) * 128) for h in hqs] + \
            [4096 + np.arange(h * 128, (h + 1) * 128) for h in hvs]
    d["cw"] = np.ascontiguousarray(np.stack([conv_w[:, ch].T for ch in chans], axis=1)).astype(np.float32)
    d["zc"] = np.stack([chunked(p_b[:, 8192 + h * 128:8192 + (h + 1) * 128]) for h in hvs])
    d["bpre"] = np.ascontiguousarray(np.stack([chunk_cols(p_b[:, 12288 + h]) for h in hvs], axis=1))
    d["apre"] = np.ascontiguousarray(np.stack([chunk_cols(p_b[:, 12320 + h]) for h in hvs], axis=1))
    d["alog"] = np.ascontiguousarray(np.broadcast_to(a_log[hvs][None, :, None], (64, 8, 65))).astype(np.float32)
    d["dtb"] = np.ascontiguousarray(np.broadcast_to(dt_bias[hvs][None, :, None], (64, 8, 65))).astype(np.float32)
    d["nw"] = np.ascontiguousarray(np.broadcast_to(norm_w[None, :], (64, 128))).astype(np.float32)
    return d


from concourse.bass_utils import run_bass_kernel_spmd
import ml_dtypes

_BF = ml_dtypes.bfloat16


def _rep(v, n=128):
    v = np.asarray(v, np.float32)
    return np.ascontiguousarray(np.broadcast_to(v[None, :], (n, v.shape[0])))


def _run(nc, ins):
    res = run_bass_kernel_spmd(nc, ins, core_ids=list(range(8)))
    return res.results


def _tok_to_batch(arrs):
    out = []
    for b in range(2):
        parts = [arrs[4 * b][1024:1040]] + [arrs[4 * b + r][0:1024] for r in range(4)]
        out.append(np.concatenate(parts, axis=0))
    return out


def _featT_to_tok(fT_b):
    outs = []
    for c in range(8):
        b, r = c // 4, c % 4
        a = fT_b[b]
        outs.append(np.ascontiguousarray(np.concatenate([a[:, 16 + r * 1024:16 + (r + 1) * 1024], a[:, 0:16]], axis=1)))
    return outs


def kernel(x, meta_tokens, norm_w, ffn_w_gate, ffn_w_up, ffn_w_down,
           mlstm_w_in, mlstm_b_if, mlstm_norm_w, mlstm_w_out,
           pool_w, pool_scale,
           gdn_w_in, gdn_conv_w, gdn_a_log, gdn_dt_bias, gdn_norm_w, gdn_w_out,
           swa_w_qkv, swa_b_qkv, swa_sinks, swa_w_out, swa_b_out, final_norm_w):
    f32 = lambda a: np.ascontiguousarray(np.asarray(a, np.float32))
    x = f32(x); meta_tokens = f32(meta_tokens); norm_w = f32(norm_w)
    wg, wu, wd = f32(ffn_w_gate), f32(ffn_w_up), f32(ffn_w_down)
    ident = np.eye(128, dtype=np.float32)

    def ffn_in(tag, i, j, nidx):
        return {"n" + tag: _rep(norm_w[i, nidx]), "wg" + tag: wg[i, j], "wu" + tag: wu[i, j], "wd" + tag: wd[i, j]}

    ins = []
    for c in range(8):
        b, r = c // 4, c % 4
        d = {"ident": ident, "h_in": np.ascontiguousarray(np.concatenate([x[b, r * 1024:(r + 1) * 1024], meta_tokens], axis=0)),
             "nU": _rep(norm_w[0, 1]), "wp": f32(mlstm_w_in[0])}
        d.update(ffn_in("B", 0, 0, 0))
        ins.append(d)
    res = _run(build_T(0, proj_cols=6160), ins)
    h = [r["h_out"] for r in res]
    p_b = _tok_to_batch([r["p_out"] for r in res])
    ins = [mlstm_inputs(p_b[c // 4], f32(mlstm_b_if[0]), f32(mlstm_norm_w[0]), c % 4) for c in range(8)]
    res = _run(build_H_mlstm(), ins)
    oT_b = [np.concatenate([np.asarray(res[4 * b + j]["oT"]) for j in range(4)], axis=0) for b in range(2)]
    oT_c = _featT_to_tok(oT_b)
    ins = []
    for c in range(8):
        d = {"ident": ident, "h_in": h[c], "oT": oT_c[c], "w_out": f32(mlstm_w_out[0]), "vec_out": np.zeros((128, 2048), np.float32),
             "nU": _rep(norm_w[1, 1])}
        d.update(ffn_in("A", 0, 1, 2)); d.update(ffn_in("B", 1, 0, 0))
        ins.append(d)
    res = _run(build_T(1, F_o=2048, proj_cols=0), ins)
    h = [r["h_out"] for r in res]
    uT_b = []
    for b in range(2):
        parts = [np.asarray(res[4 * b]["uT"])[:, 1024:1040]] + [np.asarray(res[4 * b + r]["uT"])[:, 0:1024] for r in range(4)]
        uT_b.append(np.concatenate(parts, axis=1))
    rc16 = np.zeros((128, 4, 16), np.float32)
    for g, w in enumerate((2, 4, 8, 16)):
        rc16[:, g, :] = (1.0 / np.minimum(np.arange(16) + 1, w)).astype(np.float32)
    ins = []
    for c in range(8):
        b, j = c // 4, c % 4
        rows = np.concatenate([np.arange(g * 512 + j * 128, g * 512 + j * 128 + 128) for g in range(4)])
        ins.append({"uc": np.ascontiguousarray(uT_b[b][rows]), "rc16": rc16})
    res = _run(build_H_pool(), ins)
    oT_b = []
    for b in range(2):
        a = np.zeros((2048, 4112), _BF)
        for j in range(4):
            pt = np.asarray(res[4 * b + j]["pT"])
            for g in range(4):
                a[g * 512 + j * 128:g * 512 + j * 128 + 128] = pt[g * 128:(g + 1) * 128]
        oT_b.append(a)
    oT_c = _featT_to_tok(oT_b)
    wbd = np.zeros((2048, 2048), np.float32)
    pw = f32(pool_w[0])
    for g in range(4):
        wbd[g * 512:(g + 1) * 512, g * 512:(g + 1) * 512] = pw[g]
    ins = []
    for c in range(8):
        d = {"ident": ident, "h_in": h[c], "oT": oT_c[c], "w_out": wbd, "vec_out": _rep(f32(pool_scale[0])),
             "nU": _rep(norm_w[2, 1]), "wp": f32(gdn_w_in[0])}
        d.update(ffn_in("A", 1, 1, 2)); d.update(ffn_in("B", 2, 0, 0))
        ins.append(d)
    res = _run(build_T(2, F_o=2048, proj_cols=12352), ins)
    h = [r["h_out"] for r in res]
    p_b = _tok_to_batch([r["p_out"] for r in res])
    gc_ = gdn_consts()
    ins = [gdn_inputs(p_b[c // 4], f32(gdn_conv_w[0]), f32(gdn_a_log[0]), f32(gdn_dt_bias[0]), f32(gdn_norm_w[0]), c % 4, gc_) for c in range(8)]
    res = _run(build_H_gdn(), ins)
    oT_b = [np.concatenate([np.asarray(res[4 * b + j]["oT"]) for j in range(4)], axis=0) for b in range(2)]
    oT_c = _featT_to_tok(oT_b)
    ins = []
    for c in range(8):
        d = {"ident": ident, "h_in": h[c], "oT": oT_c[c], "w_out": f32(gdn_w_out[0]), "vec_out": np.zeros((128, 2048), np.float32),
             "nU": _rep(norm_w[3, 1]), "wp": f32(swa_w_qkv[0])}
        d.update(ffn_in("A", 2, 1, 2)); d.update(ffn_in("B", 3, 0, 0))
        ins.append(d)
    res = _run(build_T(3, F_o=4096, proj_cols=2560), ins)
    h = [r["h_out"] for r in res]
    p_b = _tok_to_batch([r["p_out"] for r in res])
    sc_ = swa_consts()
    ins = [swa_inputs(p_b[c // 4], f32(swa_b_qkv[0]), f32(swa_sinks[0]), c % 4, sc_) for c in range(8)]
    res = _run(build_H_swa(), ins)
    oT_b = [np.concatenate([np.asarray(res[4 * b + j]["oT"]) for j in range(4)], axis=0) for b in range(2)]
    oT_c = _featT_to_tok(oT_b)
    ins = []
    for c in range(8):
        d = {"ident": ident, "h_in": h[c], "oT": oT_c[c], "w_out": f32(swa_w_out[0]), "vec_out": _rep(f32(swa_b_out[0])),
             "nF": _rep(f32(final_norm_w))}
        d.update(ffn_in("A", 3, 1, 2))
        ins.append(d)
    res = _run(build_T(4, F_o=2048), ins)
    out = np.zeros((2, 4096, 2048), np.float32)
    for c in range(8):
        b, r = c // 4, c % 4
        out[b, r * 1024:(r + 1) * 1024] = np.asarray(res[c]["out"])
    return out
```
